# Optimizing a Trainium2 kernel written in Bass

```python
import jax, jax.numpy as jnp
from jax import lax
import numpy as np

D_MODEL = 2048
BATCH = 4
SEQ = 8192
DEPTH = 1

CHUNK = 64
LEFT_CHUNKS = 8
BAND = (LEFT_CHUNKS + 1) * CHUNK
ATT_HEADS = 16
ATT_HEAD_DIM = 64
ATT_WIDTH = ATT_HEADS * ATT_HEAD_DIM
REL_CLIP = 256
REL_FUTURE = CHUNK - 1
N_REL = REL_CLIP + REL_FUTURE + 1
SSD_HEADS = 16
SSD_HEAD_DIM = 64
SSD_WIDTH = SSD_HEADS * SSD_HEAD_DIM
SSD_GROUPS = 2
SSD_STATE = 128
SSD_CONV = 4
SSD_CHUNK = CHUNK
XBC_WIDTH = SSD_WIDTH + 2 * SSD_GROUPS * SSD_STATE
MIX_WIDTH = ATT_WIDTH + SSD_WIDTH
IN_COLS = 3 * ATT_WIDTH + SSD_WIDTH + XBC_WIDTH + SSD_HEADS
FFN_HIDDEN = -(-8 * D_MODEL // (3 * 256)) * 256
N_MOD = 6
EPS = 1e-6

kernel_name = "hymba_chunk_attn_ssd_adaln"


def rmsnorm(x, g):
    xf = x.astype(jnp.float32)
    y = xf * lax.rsqrt(jnp.mean(xf * xf, axis=-1, keepdims=True) + EPS)
    return (y * g.astype(jnp.float32)).astype(x.dtype)


def chunk_attention(q, k, v, rel_bias):
    b, s, h, dh = q.shape
    n_chunks = s // CHUNK
    pad = LEFT_CHUNKS * CHUNK
    k_pad = jnp.pad(k, ((0, 0), (pad, 0), (0, 0), (0, 0)))
    v_pad = jnp.pad(v, ((0, 0), (pad, 0), (0, 0), (0, 0)))
    q_loc = jnp.arange(CHUNK)[:, None] + pad
    k_loc = jnp.arange(BAND)[None, :]
    rel_idx = jnp.clip(q_loc - k_loc, -REL_FUTURE, REL_CLIP) + REL_FUTURE
    bias = rel_bias[:, rel_idx].astype(jnp.float32)
    scale = ATT_HEAD_DIM ** -0.5

    def one_chunk(i):
        start = i * CHUNK
        qc = lax.dynamic_slice_in_dim(q, start, CHUNK, axis=1)
        kc = lax.dynamic_slice_in_dim(k_pad, start, BAND, axis=1)
        vc = lax.dynamic_slice_in_dim(v_pad, start, BAND, axis=1)
        scores = jnp.einsum("bqhd,bkhd->bhqk", qc, kc).astype(jnp.float32) * scale + bias
        valid = (start - pad + jnp.arange(BAND)) >= 0
        scores = jnp.where(valid, scores, -jnp.inf)
        probs = jax.nn.softmax(scores, axis=-1).astype(vc.dtype)
        return jnp.einsum("bhqk,bkhd->bqhd", probs, vc)

    out = lax.map(one_chunk, jnp.arange(n_chunks))
    return jnp.moveaxis(out, 0, 1).reshape(b, s, h * dh)


def ssd_mixer(xbc_raw, z, dt_raw, conv_w, conv_b, dt_bias, a_log, d_skip, norm_g):
    b, s, ch = xbc_raw.shape
    xbc = lax.conv_general_dilated(
        xbc_raw, conv_w[:, None, :].astype(xbc_raw.dtype), window_strides=(1,),
        padding=[(SSD_CONV - 1, 0)], dimension_numbers=("NWC", "WIO", "NWC"),
        feature_group_count=ch) + conv_b
    xbc = jax.nn.silu(xbc)
    xs, bm, cm = jnp.split(xbc, [SSD_WIDTH, SSD_WIDTH + SSD_GROUPS * SSD_STATE], axis=-1)
    nc = s // SSD_CHUNK
    r = SSD_HEADS // SSD_GROUPS
    dt = jax.nn.softplus((dt_raw + dt_bias).astype(jnp.float32))
    a = -jnp.exp(a_log.astype(jnp.float32))
    x = xs.reshape(b, nc, SSD_CHUNK, SSD_GROUPS, r, SSD_HEAD_DIM)
    bm = bm.reshape(b, nc, SSD_CHUNK, SSD_GROUPS, SSD_STATE)
    cm = cm.reshape(b, nc, SSD_CHUNK, SSD_GROUPS, SSD_STATE)
    dt = dt.reshape(b, nc, SSD_CHUNK, SSD_GROUPS, r)
    xdt = x * dt[..., None].astype(x.dtype)
    a_dt = jnp.moveaxis(dt * a.reshape(SSD_GROUPS, r), 2, -1)
    cs = jnp.cumsum(a_dt, axis=-1)
    causal = jnp.tril(jnp.ones((SSD_CHUNK, SSD_CHUNK), dtype=bool))
    seg = jnp.exp(jnp.where(causal, cs[..., :, None] - cs[..., None, :], -jnp.inf))
    cb = jnp.einsum("bclgn,bcsgn->bcgls", cm, bm)
    y_diag = jnp.einsum("bcgls,bcgrls,bcsgrp->bclgrp", cb, seg, xdt)
    decay = jnp.exp(cs[..., -1:] - cs)
    states = jnp.einsum("bclgn,bcgrl,bclgrp->bcgrpn", bm, decay, xdt).astype(jnp.float32)
    chunk_decay = jnp.exp(cs[..., -1])

    def step(h, inp):
        st, dec = inp
        return dec[..., None, None] * h + st, h

    h0 = jnp.zeros((b, SSD_GROUPS, r, SSD_HEAD_DIM, SSD_STATE), jnp.float32)
    _, prev = lax.scan(step, h0, (jnp.moveaxis(states, 1, 0), jnp.moveaxis(chunk_decay, 1, 0)))
    prev = jnp.moveaxis(prev, 0, 1)
    y_off = jnp.einsum("bclgn,bcgrpn,bcgrl->bclgrp", cm, prev, jnp.exp(cs))
    y = y_diag + y_off + x * d_skip.reshape(SSD_GROUPS, r)[:, :, None]
    y = y.reshape(b, s, SSD_WIDTH).astype(xs.dtype)
    return rmsnorm(y * jax.nn.silu(z), norm_g)


def setup_inputs(seed: int = 0) -> dict:
    key = jax.random.key(seed)
    ks = jax.random.split(key, 24)
    f32 = jnp.float32
    nrm = lambda k, shape, s: jax.random.normal(k, shape, f32) * s
    gain = lambda k, shape: 1.0 + 0.01 * jax.random.normal(k, shape, f32)
    dt0 = jnp.exp(jax.random.uniform(ks[10], (DEPTH, SSD_HEADS), f32,
                                     jnp.log(1e-3), jnp.log(1e-1)))
    return {
        "x": nrm(ks[0], (BATCH, SEQ, D_MODEL), 1.0),
        "c": nrm(ks[1], (BATCH, D_MODEL), 1.0),
        "w_ada": nrm(ks[2], (DEPTH, D_MODEL, N_MOD * D_MODEL), D_MODEL ** -0.5),
        "b_ada": nrm(ks[3], (DEPTH, N_MOD * D_MODEL), 0.01),
        "g_mix": gain(ks[4], (DEPTH, D_MODEL)),
        "w_in": nrm(ks[5], (DEPTH, D_MODEL, IN_COLS), D_MODEL ** -0.5),
        "rel_bias": nrm(ks[6], (DEPTH, ATT_HEADS, N_REL), 0.5),
        "conv_w": nrm(ks[7], (DEPTH, SSD_CONV, XBC_WIDTH), SSD_CONV ** -0.5),
        "conv_b": nrm(ks[8], (DEPTH, XBC_WIDTH), 0.01),
        "dt_bias": dt0 + jnp.log(-jnp.expm1(-dt0)),
        "a_log": jnp.log(jax.random.uniform(ks[11], (DEPTH, SSD_HEADS), f32, 1.0, 16.0)),
        "d_skip": gain(ks[12], (DEPTH, SSD_HEADS)),
        "g_att_out": gain(ks[13], (DEPTH, ATT_WIDTH)),
        "g_ssd_out": gain(ks[14], (DEPTH, SSD_WIDTH)),
        "w_out": nrm(ks[15], (DEPTH, MIX_WIDTH, D_MODEL), MIX_WIDTH ** -0.5),
        "g_ffn": gain(ks[16], (DEPTH, D_MODEL)),
        "w_gate": nrm(ks[17], (DEPTH, D_MODEL, FFN_HIDDEN), D_MODEL ** -0.5),
        "w_up": nrm(ks[18], (DEPTH, D_MODEL, FFN_HIDDEN), D_MODEL ** -0.5),
        "w_down": nrm(ks[19], (DEPTH, FFN_HIDDEN, D_MODEL), FFN_HIDDEN ** -0.5),
        "g_final": gain(ks[20], (D_MODEL,)),
    }


def reference(x, c, w_ada, b_ada, g_mix, w_in, rel_bias, conv_w, conv_b, dt_bias, a_log,
              d_skip, g_att_out, g_ssd_out, w_out, g_ffn, w_gate, w_up, w_down, g_final):
    b, s, _ = x.shape
    cond = jax.nn.silu(c)
    splits = [ATT_WIDTH, 2 * ATT_WIDTH, 3 * ATT_WIDTH, 3 * ATT_WIDTH + SSD_WIDTH,
              3 * ATT_WIDTH + SSD_WIDTH + XBC_WIDTH]
    for l in range(DEPTH):
        mods = cond @ w_ada[l] + b_ada[l]
        sh1, sc1, gt1, sh2, sc2, gt2 = [m[:, None, :] for m in jnp.split(mods, N_MOD, axis=-1)]
        h = rmsnorm(x, g_mix[l]) * (1.0 + sc1) + sh1
        proj = h @ w_in[l]
        q, k, v, z, xbc, dt_raw = jnp.split(proj, splits, axis=-1)
        att = chunk_attention(q.reshape(b, s, ATT_HEADS, ATT_HEAD_DIM),
                              k.reshape(b, s, ATT_HEADS, ATT_HEAD_DIM),
                              v.reshape(b, s, ATT_HEADS, ATT_HEAD_DIM), rel_bias[l])
        att = rmsnorm(att, g_att_out[l])
        ssd = ssd_mixer(xbc, z, dt_raw, conv_w[l], conv_b[l], dt_bias[l], a_log[l],
                        d_skip[l], g_ssd_out[l])
        mix = jnp.concatenate([att, ssd], axis=-1) @ w_out[l]
        x = x + gt1 * mix
        h = rmsnorm(x, g_ffn[l]) * (1.0 + sc2) + sh2
        ffn = (jax.nn.silu(h @ w_gate[l]) * (h @ w_up[l])) @ w_down[l]
        x = x + gt2 * ffn
    return rmsnorm(x, g_final)
```

```python
import contextlib
import numpy as np
import concourse.bass as bass
import concourse.mybir as mybir
from concourse.bass_utils import run_bass_kernel_spmd

F32, BF16 = mybir.dt.float32, mybir.dt.bfloat16
AF = mybir.ActivationFunctionType
ALU = mybir.AluOpType
ESZ = {F32: 4, BF16: 2}
EPS = 1e-6
T = 512
NEG = -30000.0
EPOCH = 30000
TRACKED = ("SBTensorHandle", "PSumTensorHandle")


class Op:
    __slots__ = ("eng", "fn", "deps", "dmadeps", "idx", "signal", "is_dma", "sem", "semval", "ordinal")


class Sched:
    def __init__(self, nc, es, nds=12):
        self.nc = nc
        self.streams = {"pe": [], "act": [], "dve": [], "pool": [], "sp": []}
        self.recs = {}
        self.dsem = {q: [es.enter_context(nc.semaphore(f"d{q}{i}")) for i in range(nds)] for q in ("pool", "sp")}
        self.dcnt = {"pool": 0, "sp": 0}
        self.dlast = {}
        self.dval = {}
        self.es = es
        self.esem = {}
        self.alt = 0

    @staticmethod
    def rect(ap):
        pairs = ap.ap
        off = ap.offset
        ps, pc = pairs[0]
        if ps == 0:
            ps = 1 << 40
        p0 = off // ps
        f0 = off % ps
        ext = 0
        for s, c in pairs[1:]:
            ext += (c - 1) * abs(s)
        e = ESZ[ap.dtype]
        return ap.tensor.name, p0, p0 + pc, f0 * e, (f0 + ext + 1) * e

    def add(self, eng, fn, reads, writes, is_dma=False):
        op = Op()
        op.eng, op.fn, op.deps, op.dmadeps, op.signal, op.is_dma = eng, fn, {}, [], False, is_dma
        op.idx = len(self.streams[eng])
        op.ordinal = 0
        for ap, isw in [(a, False) for a in reads] + [(a, True) for a in writes]:
            if ap is None or isinstance(ap, (int, float)):
                continue
            if type(ap.tensor).__name__ not in TRACKED:
                continue
            name, p0, p1, lo, hi = self.rect(ap)
            lst = self.recs.setdefault(name, [])
            keep = []
            for r in lst:
                rp0, rp1, rlo, rhi, rop, rw = r
                ov = rp0 < p1 and p0 < rp1 and rlo < hi and lo < rhi
                if ov and (rw or isw) and rop is not op:
                    if rop.is_dma:
                        if rop not in op.dmadeps:
                            op.dmadeps.append(rop)
                    elif not (rop.eng == "pe" and eng == "pe"):
                        if op.deps.get(rop.eng, -1) < rop.idx:
                            op.deps[rop.eng] = rop.idx
                cov = p0 <= rp0 and rp1 <= p1 and lo <= rlo and rhi <= hi
                if cov and (isw or (not rw and rop.eng == eng and not rop.is_dma and not is_dma)):
                    continue
                keep.append(r)
            keep.append((p0, p1, lo, hi, op, isw))
            self.recs[name] = keep
        if is_dma:
            i = self.dcnt[eng] % len(self.dsem[eng])
            self.dcnt[eng] += 1
            op.sem = self.dsem[eng][i]
            prev = self.dlast.get(op.sem)
            if prev is not None:
                op.dmadeps.append(prev)
            self.dval[op.sem] = self.dval.get(op.sem, 0) + 16
            op.semval = self.dval[op.sem]
            self.dlast[op.sem] = op
        self.streams[eng].append(op)
        return op

    def finalize(self):
        for e, lst in self.streams.items():
            for op in lst:
                for E, idx in op.deps.items():
                    self.streams[E][idx].signal = True
        for e, lst in self.streams.items():
            k = 0
            for op in lst:
                if op.signal and not op.is_dma:
                    k += 1
                    op.ordinal = k
            nep = (k + EPOCH - 1) // EPOCH
            self.esem[e] = [self.es.enter_context(self.nc.semaphore(f"c{e}{i}")) for i in range(max(nep, 1))]

    def emit(self, engname, eng):
        waited = {}
        for op in self.streams[engname]:
            for E, idx in op.deps.items():
                k = self.streams[E][idx].ordinal
                if waited.get(E, 0) >= k:
                    continue
                eng.wait_ge(self.esem[E][(k - 1) // EPOCH], (k - 1) % EPOCH + 1)
                waited[E] = k
            for d in op.dmadeps:
                if waited.get(d.sem, 0) >= d.semval:
                    continue
                eng.wait_ge(d.sem, d.semval)
                waited[d.sem] = d.semval
            ins = op.fn(eng)
            if op.is_dma:
                ins.then_inc(op.sem, 16)
            elif op.signal:
                k = op.ordinal
                ins.then_inc(self.esem[engname][(k - 1) // EPOCH], 1)

    def mm(self, out, lhsT, rhs, start=True, stop=True):
        return self.add("pe", lambda e: e.matmul(out, lhsT=lhsT, rhs=rhs, start=start, stop=stop), [lhsT, rhs], [out])

    def tr(self, out, in_, ident):
        return self.add("pe", lambda e: e.transpose(out, in_, ident), [in_, ident], [out])

    def act(self, out, in_, func, bias=0.0, scale=1.0):
        return self.add("act", lambda e: e.activation(out, in_, func, bias=bias, scale=scale), [in_, bias, scale], [out])

    def tt(self, out, in0, in1, op, eng="dve"):
        return self.add(eng, lambda e: e.tensor_tensor(out, in0, in1, op), [in0, in1], [out])

    def ts(self, out, in0, s1, s2, op0, op1=None, eng="dve"):
        if op1 is None:
            return self.add(eng, lambda e: e.tensor_scalar(out, in0, s1, None, op0), [in0, s1], [out])
        return self.add(eng, lambda e: e.tensor_scalar(out, in0, s1, s2, op0, op1), [in0, s1, s2], [out])

    def stt(self, out, in0, scalar, in1, op0, op1, eng="dve"):
        return self.add(eng, lambda e: e.scalar_tensor_tensor(out, in0, scalar, in1, op0, op1), [in0, scalar, in1], [out])

    def copy(self, out, in_, eng=None):
        if eng is None:
            self.alt ^= 1
            eng = "act" if self.alt else "dve"
        if eng == "act":
            return self.add("act", lambda e: e.activation(out, in_, AF.Copy), [in_], [out])
        return self.add(eng, lambda e: e.tensor_copy(out, in_), [in_], [out])

    def memset(self, ap, val, eng="dve"):
        return self.add(eng, lambda e: e.memset(ap, val), [], [ap])

    def dma(self, q, out, in_):
        return self.add(q, lambda e: e.dma_start(out=out, in_=in_), [in_], [out], is_dma=True)


NPV = 16 * 3 + 12 + 48 + 8 + 8 + 96
PV_GMIX, PV_GFFN, PV_GFIN, PV_CB, PV_CW, PV_GATT, PV_GSSD, PV_BADA = 0, 16, 32, 48, 60, 108, 116, 124


def build_program(NT, PRE, dbg_names=()):
    nc = bass.Bass("TRN2", target_bir_lowering=False)
    NTOK = (PRE + NT) * T

    def din(name, shape, dt=F32):
        return nc.dram_tensor(name, list(shape), dt, kind="ExternalInput").ap()

    xc = din("xc", [NTOK, 2048])
    cvec = din("cvec", [128, 16])
    w_ada = din("w_ada", [2048, 12288])
    w_in = din("w_in", [2048, 5648])
    w_out = din("w_out", [2048, 2048])
    w_gate = din("w_gate", [2048, 5632])
    w_up = din("w_up", [2048, 5632])
    w_down = din("w_down", [5632, 2048])
    pvec_d = din("pvec", [128, NPV])
    bvec_d = din("bvec", [128, 48])
    flag_d = din("flag", [128, 2])
    biasT_d = din("biasT", [8, 128, 1280])
    c32_d = din("c32", [128, 5 * 128 + 64 + 1024])
    c16_d = din("c16", [128, 4 * 128 + 1024], BF16)
    y_d = nc.dram_tensor("y", [NT * T, 2048], F32, kind="ExternalOutput").ap()
    dbg_d = {}

    es = contextlib.ExitStack()
    with es:
        S = Sched(nc, es)

        def sb(name, shape, dt):
            return es.enter_context(nc.sbuf_tensor("s_" + name, list(shape), dt))

        xT = sb("xT", [128, 16, 512], F32)
        Kb = [sb(f"K{i}", [128, 8, 512], BF16) for i in range(2)]
        Vb = [sb(f"V{i}", [128, 4, 1024], BF16) for i in range(2)]
        St = sb("St", [128, 1024], F32)
        Sbf = [sb(f"Sbf{i}", [128, 1024], BF16) for i in range(2)]
        halo = sb("halo", [128, 12, 3], F32)
        pv = sb("pv", [128, NPV], F32)
        bv = sb("bv", [128, 48], F32)
        flag = sb("flag", [128, 2], F32)
        c32 = sb("c32", [128, 5 * 128 + 64 + 1024], F32)
        c16 = sb("c16", [128, 4 * 128 + 1024], BF16)
        mods = sb("mods", [128, 96], F32)
        a12 = sb("a12", [128, 32], F32)
        a_b = sb("a_b", [128, 16], F32)
        cvs = sb("cvs", [128, 16], F32)
        condb = sb("condb", [128, 16], BF16)
        wdt = sb("wdt", [128, 16, 16], BF16)
        NW = 3
        wbuf = [sb(f"w{i}", [128, 4096], BF16) for i in range(NW)]
        RH = sb("RH", [128, 8192], BF16)
        RQ = sb("RQ", [128, 8192], BF16)
        RZ = sb("RZ", [128, 4096], BF16)
        RX = sb("RX", [128, 8192], BF16)
        RT = sb("RT", [128, 5120], BF16)
        RS = sb("RS", [128, 8576], F32)
        psum = es.enter_context(nc.psum_tensor("ps", [128, 8, 512], F32))

        ident32 = c32[:, 0:128]
        blockones = c32[:, 128:256]
        BT2 = c32[:, 256:384]
        selA = c32[:, 384:512]
        selB = c32[:, 512:640]
        tri2 = c32[:, 640:704]
        negmask = c32[:, 704:1728]
        ident16 = c16[:, 0:128]
        ones16 = c16[:, 128:256]
        onesD = c16[:, 256:384]
        onesH = c16[:, 384:512]
        cmask = c16[:, 512:1536]
        gmix, gffn, gfin = pv[:, 0:16], pv[:, 16:32], pv[:, 32:48]
        convb = pv[:, PV_CB:PV_CB + 12]
        convw = pv[:, PV_CW:PV_CW + 48]
        gatt, gssd = pv[:, PV_GATT:PV_GATT + 8], pv[:, PV_GSSD:PV_GSSD + 8]
        bada = pv[:, PV_BADA:PV_BADA + 96]
        dtb, alog, dskip = bv[:, 0:16], bv[:, 16:32], bv[:, 32:48]
        sh1, sc1, gt1 = mods[:, 0:16], mods[:, 16:32], mods[:, 32:48]
        sh2, sc2, gt2 = mods[:, 48:64], mods[:, 64:80], mods[:, 80:96]
        a1, a2 = a12[:, 0:16], a12[:, 16:32]

        hT = RH[:, :].rearrange("p (a b) -> p a b", a=16)
        attT = RH[:, 0:4096].rearrange("p (a b) -> p a b", a=8)
        gyT = RH[:, 4096:8192].rearrange("p (a b) -> p a b", a=8)
        qbd = RQ[:, :].rearrange("p (h j c) -> p h j c", h=8, j=8)
        mixT = RQ[:, :].rearrange("p (a b) -> p a b", a=16)
        szT = RZ[:, :].rearrange("p (a b) -> p a b", a=8)
        actT = [RZ[:, i * 2048:(i + 1) * 2048].rearrange("p (a b) -> p a b", a=4) for i in range(2)]
        xsT = RX[:, 0:6144].rearrange("p (a b) -> p a b", a=12)
        CTpad = RX[:, 6144:8192].rearrange("p (c g t) -> p c g t", c=2, g=2)
        RXf = RX[:, :].bitcast(F32)
        sgt = [RXf[:, i * 512:(i + 1) * 512] for i in range(4)]
        xtok = RT[:, 0:4096].rearrange("p (a b) -> p a b", a=4)
        Btok = RT[:, 4096:5120].rearrange("p (a b) -> p a b", a=4)
        TA = RS[:, 0:1024]
        TB = RS[:, 1024:2048]
        TC = RS[:, 2048:3072]
        xst = [TA, TB, TC]
        RSb = RS[:, :].bitcast(BF16)
        Mbd = RSb[:, 6144:8192].rearrange("p (r c) -> p r c", r=16)
        seg = RSb[:, 8192:9216]
        xdt = RSb[:, 9216:10240]
        xdtw = RSb[:, 10240:11264]
        rstd = RS[:, 5632:6144]
        rstd2 = RS[:, 6144:6656]
        small = RS[:, 6656:7168]
        cbsb = RS[:, 7168:7296].rearrange("p (g l) -> p g l", g=2)
        sqr = [RSb[:, 14592 + i * 512:14592 + (i + 1) * 512] for i in range(3)]
        tmpf = [RS[:, 8064:8576], None]
        pTt = [RSb[:, 11264 + k * 640:11264 + (k + 1) * 640] for k in range(3)]
        pTt.append(RSb[:, 4608:5248])
        recf = [RS[:, 2048:2176], RS[:, 2176:2304]]
        sbt = [RXf[:, k * 640:(k + 1) * 640] for k in range(3)]
        biasb = [RS[:, 7296:7936], RS[:, 7936:8576]]
        xbcr = [RS[:, 0:515], RS[:, 515:1030]]
        cacc = [RS[:, 1030:1542], RS[:, 1542:2054]]
        tmpf[1] = RS[:, 3072:3584]

        bankctr = [0]

        def bank():
            i = bankctr[0] % 6
            bankctr[0] += 1
            return psum[:, i, :]

        wctr = [0]

        def wload(src, shape3):
            b = wbuf[wctr[0] % NW]
            wctr[0] += 1
            a_, b_ = shape3
            view = b[:, 0:a_ * b_].rearrange("p (a b) -> p a b", a=a_)
            S.dma("pool", view, src)
            return view

        def dbg(name, ap, shape):
            if name in dbg_names:
                d = nc.dram_tensor("dbg_" + name, list(shape), ap.dtype, kind="ExternalOutput").ap()
                dbg_d[name] = d
                S.dma("sp", d, ap)

        S.dma("sp", pv[:], pvec_d)
        S.dma("sp", bv[:], bvec_d)
        S.dma("sp", flag[:], flag_d)
        S.dma("sp", c32[:], c32_d)
        S.dma("sp", c16[:], c16_d)
        S.dma("sp", cvs[:], cvec)
        S.dma("pool", wdt[:], w_in[:, 5632:5648].rearrange("(kc p) n -> p kc n", p=128))
        S.memset(halo[:], 0.0)
        S.memset(St[:], 0.0)
        for i in range(2):
            S.memset(Kb[i][:], 0.0)
            S.memset(Vb[i][:], 0.0)
        S.act(condb[:], cvs[:], AF.Silu)
        mps = bank()
        wav = w_ada.rearrange("(kc p) n -> p kc n", p=128)
        for u in range(48):
            wb = wload(wav[:, :, u * 256:(u + 1) * 256], (16, 256))
            for c2 in range(2):
                col = u * 2 + c2
                for kc in range(16):
                    S.mm(mps[:, col:col + 1], wb[:, kc, c2 * 128:(c2 + 1) * 128], condb[:, kc:kc + 1],
                         start=(kc == 0), stop=(kc == 15))
        S.tt(mods[:], mps[:, 0:96], bada, ALU.add)
        S.stt(a1, sc1, 1.0, gmix, ALU.add, ALU.mult)
        S.stt(a2, sc2, 1.0, gffn, ALU.add, ALU.mult)
        S.act(a_b[:], alog, AF.Exp)
        S.ts(a_b[:], a_b[:], -1.0, None, ALU.mult)
        dbg("mods", mods[:], [128, 96])

        def stats(src3, nk, ones_ap, out_rstd):
            sp_ = bank()
            for kc in range(nk):
                sq = sqr[kc % 3]
                S.act(sq, src3[:, kc, :], AF.Square)
                S.mm(sp_, ones_ap, sq, start=(kc == 0), stop=(kc == nk - 1))
            S.act(out_rstd, sp_, AF.Ln, bias=EPS)
            S.act(out_rstd, out_rstd, AF.Exp, scale=-0.5)

        def load_x(t0):
            for _ in load_x_gen(t0):
                pass

        def load_x_gen(t0):
            for blk in range(4):
                for half in range(2):
                    st = xst[(blk * 2 + half) % 3]
                    S.dma("sp", st, xc[t0 + blk * 128:t0 + (blk + 1) * 128, half * 1024:(half + 1) * 1024])
                    for g in range(2):
                        pb = bank()
                        for i in range(4):
                            S.tr(pb[:, i * 128:(i + 1) * 128], st[:, (g * 4 + i) * 128:(g * 4 + i + 1) * 128], ident32)
                        kc0 = half * 8 + g * 4
                        S.copy(xT[:, kc0:kc0 + 4, blk * 128:(blk + 1) * 128], pb.rearrange("p (a b) -> p a b", a=4))
                        yield

        def front_gen(ti):
            yield from load_x_gen(ti * T)
            sp_ = bank()
            for kc in range(16):
                sq = sqr[kc % 3]
                S.act(sq, xT[:, kc, :], AF.Square)
                S.mm(sp_, onesD, sq, start=(kc == 0), stop=(kc == 15))
                if kc % 4 == 3:
                    yield
            S.act(rstd, sp_, AF.Ln, bias=EPS)
            S.act(rstd, rstd, AF.Exp, scale=-0.5)
            yield
            for kc in range(16):
                tm = tmpf[kc % 2]
                S.stt(tm, xT[:, kc, :], a1[:, kc:kc + 1], rstd, ALU.mult, ALU.mult)
                S.act(hT[:, kc, :], tm, AF.Identity, bias=sh1[:, kc:kc + 1])
                if kc % 2 == 1:
                    yield

        def make_h(rs, a_, sh_):
            for kc in range(16):
                tm = tmpf[kc % 2]
                S.stt(tm, xT[:, kc, :], a_[:, kc:kc + 1], rs, ALU.mult, ALU.mult)
                S.act(hT[:, kc, :], tm, AF.Identity, bias=sh_[:, kc:kc + 1])

        winv = w_in.rearrange("(kc p) n -> p kc n", p=128)

        def inproj(ti, mode):
            cur = ti % 2
            if mode == "main":
                S.memset(RQ[:, :], 0.0)
            units = list(range(22)) if mode == "main" else (
                list(range(16, 21)) if mode == "pre" else list(range(4, 12)) + list(range(16, 22)))
            for u in units:
                wb = wload(winv[:, :, u * 256:(u + 1) * 256], (16, 256))
                if 8 <= u < 12:
                    for blk in range(4):
                        pb = bank()
                        for kc in range(16):
                            S.mm(pb[:, 0:256], hT[:, kc, blk * 128:(blk + 1) * 128], wb[:, kc, :],
                                 start=(kc == 0), stop=(kc == 15))
                        S.copy(Vb[cur][:, blk, (u - 8) * 256:(u - 7) * 256], pb[:, 0:256])
                    continue
                for c2 in range(2):
                    cc = u * 2 + c2
                    pb = bank()
                    for kc in range(16):
                        S.mm(pb, wb[:, kc, c2 * 128:(c2 + 1) * 128], hT[:, kc, :], start=(kc == 0), stop=(kc == 15))
                    if cc < 8:
                        hp = cc
                        S.ts(qbd[0:64, hp, :, 0:64], pb[0:64, :].rearrange("p (j q) -> p j q", j=8), 0.125, None, ALU.mult)
                        S.ts(qbd[64:128, hp, :, 64:128], pb[64:128, :].rearrange("p (j q) -> p j q", j=8), 0.125, None,
                             ALU.mult)
                    elif cc < 16:
                        S.copy(Kb[cur][:, cc - 8, :], pb)
                    elif cc < 32:
                        S.act(szT[:, cc - 24, :], pb, AF.Silu)
                    else:
                        xi = cc - 32
                        raw = xbcr[xi % 2]
                        acc = cacc[xi % 2]
                        S.copy(raw[:, 3:515], pb, eng="act")
                        S.copy(raw[:, 0:3], halo[:, xi, :], eng="dve")
                        S.ts(acc, raw[:, 0:512], convw[:, xi * 4:xi * 4 + 1], convb[:, xi:xi + 1], ALU.mult, ALU.add)
                        for j in range(1, 4):
                            S.stt(acc, raw[:, j:j + 512], convw[:, xi * 4 + j:xi * 4 + j + 1], acc, ALU.mult, ALU.add)
                        S.copy(halo[:, xi, :], raw[:, 512:515], eng="dve")
                        S.act(xsT[:, xi, :], acc, AF.Silu)
            dps = bank()
            for blk in range(4):
                for kc in range(16):
                    S.mm(dps[:, blk * 16:(blk + 1) * 16], hT[:, kc, blk * 128:(blk + 1) * 128], wdt[:, kc, :],
                         start=(kc == 0), stop=(kc == 15))
            dsb = small[:, 208:272]
            S.copy(dsb, dps[:, 0:64], eng="dve")
            return dsb

        def bc(ap2, n):
            return ap2.unsqueeze(2).broadcast_to([ap2.shape[0], ap2.shape[1], n])

        sbctr = [0]

        def sbank():
            sbctr[0] += 1
            return psum[:, 6 + sbctr[0] % 2, :]

        def sbank2():
            return psum[:, 6:8, :].rearrange("p a b -> p (a b)")

        def ssd_gen(ti, mode, dps):
            main = mode == "main"
            if main:
                S.memset(Mbd, 0.0, eng="pool")
            for blk in range(4):
                pb = sbank().bitcast(BF16)
                for fc in range(8):
                    S.tr(pb[:, fc * 128:(fc + 1) * 128], xsT[:, fc, blk * 128:(blk + 1) * 128], ident16)
                S.copy(xtok[:, blk, :], pb)
                pb2 = sbank().bitcast(BF16)
                for g in range(2):
                    S.tr(pb2[:, g * 128:(g + 1) * 128], xsT[:, 8 + g, blk * 128:(blk + 1) * 128], ident16)
                S.copy(Btok[:, blk, :], pb2[:, 0:256])
                yield
            if main:
                for c in range(2):
                    S.tt(CTpad[:, c, :, :], xsT[:, 10:12, :],
                         cmask[:, c * 512:(c + 1) * 512].unsqueeze(1).broadcast_to([128, 2, 512]), ALU.mult, eng="pool")
            dtv, ev, dt_, adt = small[:, 0:16], small[:, 16:32], small[:, 32:48], small[:, 48:64]
            cs4 = small[:, 64:128]
            dl, dec, cdAB, ecs = small[:, 128:144], small[:, 144:160], small[:, 160:192], small[:, 192:208]
            St3 = St[:, :].rearrange("p (r c) -> p r c", r=16)
            r16 = lambda ap: ap.rearrange("p (r c) -> p r c", r=16)
            for blk in range(4):
                tk = slice(blk * 128, (blk + 1) * 128)
                S.tt(dtv, dps[:, blk * 16:(blk + 1) * 16], dtb, ALU.add)
                S.act(ev, dtv, AF.Exp)
                S.act(dt_, ev, AF.Ln, bias=1.0)
                S.tt(adt, dt_, a_b[:], ALU.mult)
                yield
                cp = sbank()
                S.mm(cp[:, 0:16], BT2, adt)
                S.mm(cp[:, 16:32], blockones, adt)
                S.mm(cp[:, 32:48], selA, adt)
                S.mm(cp[:, 48:64], selB, adt)
                S.copy(cs4, cp[:, 0:64], eng="dve")
                S.tt(dl, cs4[:, 16:32], cs4[:, 0:16], ALU.subtract)
                S.act(dec, dl, AF.Exp)
                S.act(cdAB, cs4[:, 32:64], AF.Exp)
                if main:
                    S.act(ecs, cs4[:, 0:16], AF.Exp)
                yield
                xtok3 = r16(xtok[:, blk, :])
                S.tt(r16(xdt), xtok3, bc(dt_, 64), ALU.mult, eng="pool")
                S.tt(r16(xdtw), r16(xdt), bc(dec, 64), ALU.mult, eng="pool")
                yield
                if main:
                    cbp = sbank()
                    for g in range(2):
                        S.mm(cbp[:, g * 128:(g + 1) * 128], xsT[:, 8 + g, tk], xsT[:, 10 + g, tk])
                    cb3 = cbp[:, 0:256].rearrange("p (g l) -> p g l", g=2)
                    S.copy(cbsb[0:64, :, :], cb3[0:64, :, 0:64], eng="dve")
                    S.copy(cbsb[64:128, :, :], cb3[64:128, :, 64:128], eng="dve")
                    rhsD = TA
                    S.tt(r16(rhsD), tri2.unsqueeze(1).broadcast_to([128, 16, 64]), bc(adt, 64), ALU.mult, eng="pool")
                    yield
                    dp = sbank2()
                    for h2 in range(2):
                        S.mm(dp[:, h2 * 512:(h2 + 1) * 512], blockones, rhsD[:, h2 * 512:(h2 + 1) * 512], True, False)
                        S.mm(dp[:, h2 * 512:(h2 + 1) * 512], ident32, negmask[:, h2 * 512:(h2 + 1) * 512], False, True)
                    darg = TB
                    S.tt(r16(darg), r16(dp), bc(cs4[:, 0:16], 64), ALU.subtract)
                    S.act(seg, darg, AF.Exp)
                    yield
                    seg4 = seg.rearrange("p (g r c) -> p g r c", g=2, r=8)
                    Mbd4 = Mbd.rearrange("p (g r) c -> p g r c", g=2)
                    for c in range(2):
                        pr = slice(c * 64, (c + 1) * 64)
                        S.tt(Mbd4[pr, :, :, c * 64:(c + 1) * 64], seg4[pr],
                             cbsb[pr, :, :].unsqueeze(2).broadcast_to([64, 2, 8, 64]), ALU.mult)
                    yp = sbank2()
                    for r in range(16):
                        S.mm(yp[:, r * 64:(r + 1) * 64], Mbd[:, r, :], xdt[:, r * 64:(r + 1) * 64])
                    t1 = TA
                    S.copy(t1, yp, eng="act")
                    S.copy(Sbf[0][:], St[:], eng="act")
                    yield
                spA = sbank2()
                for g in range(2):
                    S.mm(spA[:, g * 512:(g + 1) * 512], Btok[0:64, blk, g * 128:(g + 1) * 128], xdtw[0:64, g * 512:(g + 1) * 512])
                S.tt(St3, St3, bc(cdAB[:, 0:16], 64), ALU.mult, eng="pool")
                S.tt(St[:], St[:], spA, ALU.add)
                yield
                if main:
                    S.copy(Sbf[1][:], St[:], eng="act")
                    op_ = sbank2()
                    for g in range(2):
                        S.mm(op_[:, g * 512:(g + 1) * 512], CTpad[:, 0, g, tk], Sbf[0][:, g * 512:(g + 1) * 512], True, False)
                        S.mm(op_[:, g * 512:(g + 1) * 512], CTpad[:, 1, g, tk], Sbf[1][:, g * 512:(g + 1) * 512], False, True)
                    t2 = TB
                    S.tt(r16(t2), r16(op_), bc(ecs, 64), ALU.mult)
                    S.tt(t1, t1, t2, ALU.add, eng="pool")
                    yield
                spB = sbank2()
                for g in range(2):
                    S.mm(spB[:, g * 512:(g + 1) * 512], Btok[64:128, blk, g * 128:(g + 1) * 128],
                         xdtw[64:128, g * 512:(g + 1) * 512])
                S.tt(St3, St3, bc(cdAB[:, 16:32], 64), ALU.mult, eng="pool")
                S.tt(St[:], St[:], spB, ALU.add)
                yield
                if main:
                    S.tt(r16(t2), xtok3, bc(dskip, 64), ALU.mult, eng="pool")
                    S.tt(t1, t1, t2, ALU.add, eng="pool")
                    if blk == 0 and ti == 0:
                        dbg("ytok", t1, [128, 1024])
                    tp = sbank2()
                    for fc in range(8):
                        S.tr(tp[:, fc * 128:(fc + 1) * 128], t1[:, fc * 128:(fc + 1) * 128], ident32)
                    S.tt(gyT[:, :, tk], tp.rearrange("p (a b) -> p a b", a=8), szT[:, :, tk], ALU.mult)
                    yield

        def attention(ti, first_main, steps):
            cur = ti % 2
            prev = 1 - cur
            its = [(hp, par, j) for hp in range(8) for par in range(2) for j in range(par, 8, 2)]
            N = len(its)

            def rows_of(par, m):
                if par == 0 and m == 4:
                    return slice(0, 64)
                if par == 1 and m == 0:
                    return slice(64, 128)
                return slice(0, 128)

            def stageA(i):
                hp, par, j = its[i]
                grp = i // 4
                bb = biasb[grp % 2]
                if i % 4 == 0:
                    S.dma("sp", bb, biasT_d[hp][:, par * 640:(par + 1) * 640])
                b0 = j // 2
                sps = psum[:, 2 * (i % 2):2 * (i % 2) + 2, :].rearrange("p a b -> p (a b)")
                for m in range(5):
                    b = b0 + m
                    kb = Kb[prev] if b < 4 else Kb[cur]
                    S.mm(sps[:, m * 128:(m + 1) * 128], kb[:, hp, (b % 4) * 128:(b % 4 + 1) * 128], qbd[:, hp, j, :])
                sbb = sbt[i % 3]
                S.copy(sbb, sps[:, 0:640], eng="act")
                S.tt(sbb, sbb, bb, ALU.add, eng="pool")
                pT = pTt[i % 4]
                npv = 4 - b0
                if first_main:
                    S.act(pT[:, 0:npv * 128], sbb[:, 0:npv * 128], AF.Exp, bias=flag[:, 1:2])
                    S.act(pT[:, npv * 128:640], sbb[:, npv * 128:640], AF.Exp)
                else:
                    S.act(pT, sbb, AF.Exp)

            def stageB(i):
                hp, par, j = its[i]
                b0 = j // 2
                pT = pTt[i % 4]
                ob = psum[:, 4 + i % 2, :]
                for m in range(5):
                    b = b0 + m
                    vb = Vb[prev] if b < 4 else Vb[cur]
                    rows = rows_of(par, m)
                    S.mm(ob[:, 0:128], vb[rows, b % 4, hp * 128:(hp + 1) * 128], pT[rows, m * 128:(m + 1) * 128],
                         start=(m == 0), stop=(m == 4))
                for m in range(5):
                    rows = rows_of(par, m)
                    S.mm(ob[:, 128:256], ones16[rows, :], pT[rows, m * 128:(m + 1) * 128], start=(m == 0), stop=(m == 4))
                rc = recf[i % 2]
                S.add("dve", lambda e, rc=rc, ob=ob: e.reciprocal(rc, ob[:, 128:256]), [ob[:, 128:256]], [rc])
                S.tt(attT[0:64, hp, j * 64:(j + 1) * 64], ob[0:64, 0:64], rc[0:64, 0:64], ALU.mult)
                S.tt(attT[64:128, hp, j * 64:(j + 1) * 64], ob[64:128, 64:128], rc[64:128, 64:128], ALU.mult)

            LAG = 3
            for i in range(N + LAG):
                if i < N:
                    stageA(i)
                if i >= LAG:
                    stageB(i - LAG)
                next(steps, None)
            for _ in steps:
                pass

        def mixer_out():
            stats(attT, 8, onesH, rstd)
            stats(gyT, 8, onesH, rstd2)
            for fc in range(8):
                S.stt(mixT[:, fc, :], attT[:, fc, :], gatt[:, fc:fc + 1], rstd, ALU.mult, ALU.mult)
            for fc in range(8):
                S.stt(mixT[:, 8 + fc, :], gyT[:, fc, :], gssd[:, fc:fc + 1], rstd2, ALU.mult, ALU.mult)
            wov = w_out.rearrange("(kc p) n -> p kc n", p=128)
            for u in range(8):
                wb = wload(wov[:, :, u * 256:(u + 1) * 256], (16, 256))
                for c2 in range(2):
                    c = u * 2 + c2
                    pb = bank()
                    for kc in range(16):
                        S.mm(pb, wb[:, kc, c2 * 128:(c2 + 1) * 128], mixT[:, kc, :], start=(kc == 0), stop=(kc == 15))
                    S.stt(xT[:, c, :], pb, gt1[:, c:c + 1], xT[:, c, :], ALU.mult, ALU.add)

        wgv = w_gate.rearrange("(kc p) n -> p kc n", p=128)
        wuv = w_up.rearrange("(kc p) n -> p kc n", p=128)

        def ffn():
            stats(xT, 16, onesD, rstd)
            make_h(rstd, a2, sh2)
            for g in range(11):
                at = actT[g % 2]
                for half in range(2):
                    u = g * 2 + half
                    wg = wload(wgv[:, :, u * 256:(u + 1) * 256], (16, 256))
                    wu = wload(wuv[:, :, u * 256:(u + 1) * 256], (16, 256))
                    for c2 in range(2):
                        gp = bank()
                        for kc in range(16):
                            S.mm(gp, wg[:, kc, c2 * 128:(c2 + 1) * 128], hT[:, kc, :], start=(kc == 0), stop=(kc == 15))
                        up = bank()
                        for kc in range(16):
                            S.mm(up, wu[:, kc, c2 * 128:(c2 + 1) * 128], hT[:, kc, :], start=(kc == 0), stop=(kc == 15))
                        sg = sgt[(half * 2 + c2) % 4]
                        S.act(sg, gp, AF.Silu)
                        S.tt(at[:, half * 2 + c2, :], sg, up, ALU.mult)
                for oh in range(2):
                    wd = wload(w_down[g * 512:(g + 1) * 512, oh * 1024:(oh + 1) * 1024].rearrange("(kc p) n -> p kc n", p=128),
                               (4, 1024))
                    for o8 in range(8):
                        oc = oh * 8 + o8
                        pb = bank()
                        for kc in range(4):
                            S.mm(pb, wd[:, kc, o8 * 128:(o8 + 1) * 128], at[:, kc, :], start=(kc == 0), stop=(kc == 3))
                        S.stt(xT[:, oc, :], pb, gt2[:, oc:oc + 1], xT[:, oc, :], ALU.mult, ALU.add)

        def final_out(tm):
            stats(xT, 16, onesD, rstd)
            for kc in range(16):
                S.stt(xT[:, kc, :], xT[:, kc, :], gfin[:, kc:kc + 1], rstd, ALU.mult, ALU.mult)
            for blk in range(4):
                for half in range(2):
                    st = xst[(blk * 2 + half) % 3]
                    for g in range(2):
                        pb = bank()
                        for i in range(4):
                            kc = half * 8 + g * 4 + i
                            S.tr(pb[:, i * 128:(i + 1) * 128], xT[:, kc, blk * 128:(blk + 1) * 128], ident32)
                        S.copy(st[:, g * 512:(g + 1) * 512], pb)
                    S.dma("sp", y_d[tm * T + blk * 128:tm * T + (blk + 1) * 128, half * 1024:(half + 1) * 1024], st)

        front_done = set()
        for ti in range(PRE + NT):
            mode = "main" if ti >= PRE else ("prelast" if ti == PRE - 1 else "pre")
            if ti not in front_done:
                for _ in front_gen(ti):
                    pass
            if ti == 0:
                dbg("hT", hT, [128, 16, 512])
            dps = inproj(ti, mode)
            if ti == 0:
                dbg("xsT", xsT, [128, 12, 512])
                dbg("szT", szT, [128, 8, 512])
                dbg("qbd", qbd, [128, 8, 8, 128])
                dbg("K", Kb[0][:], [128, 8, 512])
                dbg("V", Vb[0][:], [128, 4, 1024])
            steps = ssd_gen(ti, mode, dps)
            if mode != "main":
                nxt = front_gen(ti + 1)
                front_done.add(ti + 1)
                a_live, b_live = True, True
                while a_live or b_live:
                    if a_live:
                        a_live = next(steps, "END") != "END"
                    if b_live:
                        b_live = next(nxt, "END") != "END"
            if ti == PRE - 1:
                S.ts(St[:], St[:], flag[:, 0:1], None, ALU.mult)
                S.ts(halo[:], halo[:], flag[:, 0:1], None, ALU.mult)
            if mode == "main":
                for _ in range(4):
                    next(steps)
                attention(ti, ti == PRE, steps)
                if ti == PRE:
                    dbg("gyT", gyT, [128, 8, 512])
                if ti == PRE:
                    dbg("attT", attT, [128, 8, 512])
                mixer_out()
                if ti == PRE:
                    dbg("x1", xT[:], [128, 16, 512])
                ffn()
                final_out(ti - PRE)

        S.finalize()
        with nc.Block() as block:
            @block.tensor
            def _(e):
                S.emit("pe", e)

            @block.scalar
            def _(e):
                S.emit("act", e)

            @block.vector
            def _(e):
                S.emit("dve", e)

            @block.gpsimd
            def _(e):
                S.emit("pool", e)

            @block.sync
            def _(e):
                S.emit("sp", e)
                for sem, val in S.dval.items():
                    e.wait_ge(sem, val)
    return nc, list(dbg_d.keys())


def host_consts():
    import ml_dtypes
    c32 = np.zeros((128, 5 * 128 + 64 + 1024), np.float32)
    t = np.arange(128)
    c32[:, 0:128] = np.eye(128)
    same = (t[:, None] // 64) == (t[None, :] // 64)
    c32[:, 128:256] = same
    c32[:, 256:384] = same & (t[:, None] <= t[None, :])
    c32[:, 384:512] = (t[:, None] < 64) * np.ones((1, 128))
    c32[:, 512:640] = (t[:, None] >= 64) * np.ones((1, 128))
    l = np.arange(64)
    c32[:, 640:704] = (t[:, None] % 64) <= l[None, :]
    nm = np.where(l[None, :] < (t[:, None] % 64), NEG, 0.0)
    c32[:, 704:1728] = np.tile(nm, (1, 16))
    c16 = np.zeros((128, 4 * 128 + 1024), np.float32)
    c16[:, 0:128] = np.eye(128)
    c16[:, 128:256] = 1.0
    c16[:, 256:384] = 1.0 / 2048
    c16[:, 384:512] = 1.0 / 1024
    tok = np.arange(512)
    c16[:, 512:1024] = ((tok // 64) % 2 == 0)[None, :]
    c16[:, 1024:1536] = ((tok // 64) % 2 == 1)[None, :]
    return c32, c16.astype(ml_dtypes.bfloat16)


def host_bias(rel_bias):
    rb = np.asarray(rel_bias, np.float32)
    p = np.arange(128)[:, None, None, None]
    par = np.arange(2)[None, :, None, None]
    m = np.arange(5)[None, None, :, None]
    q = np.arange(64)[None, None, None, :]
    rel = 512 + 64 * par + q - 128 * m - p
    idx = np.clip(rel, -63, 256) + 63
    out = np.zeros((8, 128, 2, 5, 2, 64), np.float32)
    for hp in range(8):
        for hh in range(2):
            out[hp, :, :, :, hh, :] = rb[2 * hp + hh][idx]
    return out.reshape(8, 128, 1280)


def pcol(v, n):
    return np.ascontiguousarray(np.asarray(v, np.float32).reshape(n, 128).T)


def make_inmaps(inputs, NT, PRE, cores):
    c32, c16 = host_consts()
    biasT = host_bias(inputs["rel_bias"][0])
    pvec = np.zeros((128, NPV), np.float32)
    pvec[:, 0:16] = pcol(inputs["g_mix"][0], 16)
    pvec[:, 16:32] = pcol(inputs["g_ffn"][0], 16)
    pvec[:, 32:48] = pcol(inputs["g_final"], 16)
    pvec[:, PV_CB:PV_CB + 12] = pcol(inputs["conv_b"][0], 12)
    cw = np.asarray(inputs["conv_w"][0], np.float32)
    for xi in range(12):
        for j in range(4):
            pvec[:, PV_CW + xi * 4 + j] = cw[j, xi * 128:(xi + 1) * 128]
    pvec[:, PV_GATT:PV_GATT + 8] = pcol(inputs["g_att_out"][0], 8)
    pvec[:, PV_GSSD:PV_GSSD + 8] = pcol(inputs["g_ssd_out"][0], 8)
    pvec[:, PV_BADA:PV_BADA + 96] = pcol(inputs["b_ada"][0], 96)
    bvec = np.zeros((128, 48), np.float32)
    bvec[:, 0:16] = np.asarray(inputs["dt_bias"][0], np.float32)[None, :]
    bvec[:, 16:32] = np.asarray(inputs["a_log"][0], np.float32)[None, :]
    bvec[:, 32:48] = np.asarray(inputs["d_skip"][0], np.float32)[None, :]
    shared = dict(
        w_ada=np.ascontiguousarray(inputs["w_ada"][0]), w_in=np.ascontiguousarray(inputs["w_in"][0]),
        w_out=np.ascontiguousarray(inputs["w_out"][0]), w_gate=np.ascontiguousarray(inputs["w_gate"][0]),
        w_up=np.ascontiguousarray(inputs["w_up"][0]), w_down=np.ascontiguousarray(inputs["w_down"][0]),
        pvec=pvec, bvec=bvec, biasT=biasT, c32=c32, c16=c16)
    x = inputs["x"]
    maps = []
    for (b, t0) in cores:
        xcat = np.zeros(((PRE + NT) * T, 2048), np.float32)
        has_prev = t0 > 0
        if has_prev and PRE > 0:
            xcat[0:PRE * T] = x[b, t0 - PRE * T:t0]
        xcat[PRE * T:] = x[b, t0:t0 + NT * T]
        flag = np.zeros((128, 2), np.float32)
        flag[:, 0] = 1.0 if has_prev else 0.0
        flag[:, 1] = 0.0 if has_prev else NEG
        m = dict(shared)
        m.update(xc=xcat, cvec=pcol(inputs["c"][b], 16), flag=flag)
        maps.append(m)
    return maps


_CACHE = {}


def kernel(**inputs):
    inputs = {k: np.asarray(v) for k, v in inputs.items()}
    NT, PRE = 8, 8
    cores = [(b, h * 4096) for b in range(4) for h in range(2)]
    if "nc" not in _CACHE:
        _CACHE["nc"] = build_program(NT, PRE)[0]
    nc = _CACHE["nc"]
    maps = make_inmaps(inputs, NT, PRE, cores)
    res = run_bass_kernel_spmd(nc, maps, core_ids=list(range(8)))
    out = np.zeros((4, 8192, 2048), np.float32)
    for i, (b, t0) in enumerate(cores):
        out[b, t0:t0 + NT * T] = res.results[i]["y"]
    return out
```

```python
import contextlib
import numpy as np
import concourse.bass as bass
import concourse.mybir as mybir
from concourse.bass_utils import run_bass_kernel_spmd

F32, BF16 = mybir.dt.float32, mybir.dt.bfloat16
AF = mybir.ActivationFunctionType
ALU = mybir.AluOpType
ESZ = {F32: 4, BF16: 2}
EPS = 1e-6
T = 512
NEG = -30000.0
EPOCH = 30000
TRACKED = ("SBTensorHandle", "PSumTensorHandle")


class Op:
    __slots__ = ("eng", "fn", "deps", "dmadeps", "idx", "signal", "is_dma", "sem", "semval", "ordinal")


class Sched:
    def __init__(self, nc, es, nds=12):
        self.nc = nc
        self.streams = {"pe": [], "act": [], "dve": [], "pool": [], "sp": []}
        self.recs = {}
        self.dsem = {q: [es.enter_context(nc.semaphore(f"d{q}{i}")) for i in range(nds)] for q in ("pool", "sp")}
        self.dcnt = {"pool": 0, "sp": 0}
        self.dlast = {}
        self.dval = {}
        self.es = es
        self.esem = {}
        self.alt = 0

    @staticmethod
    def rect(ap):
        pairs = ap.ap
        off = ap.offset
        ps, pc = pairs[0]
        if ps == 0:
            ps = 1 << 40
        p0 = off // ps
        f0 = off % ps
        ext = 0
        for s, c in pairs[1:]:
            ext += (c - 1) * abs(s)
        e = ESZ[ap.dtype]
        return ap.tensor.name, p0, p0 + pc, f0 * e, (f0 + ext + 1) * e

    def add(self, eng, fn, reads, writes, is_dma=False):
        op = Op()
        op.eng, op.fn, op.deps, op.dmadeps, op.signal, op.is_dma = eng, fn, {}, [], False, is_dma
        op.idx = len(self.streams[eng])
        op.ordinal = 0
        for ap, isw in [(a, False) for a in reads] + [(a, True) for a in writes]:
            if ap is None or isinstance(ap, (int, float)):
                continue
            if type(ap.tensor).__name__ not in TRACKED:
                continue
            name, p0, p1, lo, hi = self.rect(ap)
            lst = self.recs.setdefault(name, [])
            keep = []
            for r in lst:
                rp0, rp1, rlo, rhi, rop, rw = r
                ov = rp0 < p1 and p0 < rp1 and rlo < hi and lo < rhi
                if ov and (rw or isw) and rop is not op:
                    if rop.is_dma:
                        if rop not in op.dmadeps:
                            op.dmadeps.append(rop)
                    elif not (rop.eng == "pe" and eng == "pe"):
                        if op.deps.get(rop.eng, -1) < rop.idx:
                            op.deps[rop.eng] = rop.idx
                cov = p0 <= rp0 and rp1 <= p1 and lo <= rlo and rhi <= hi
                if cov and (isw or (not rw and rop.eng == eng and not rop.is_dma and not is_dma)):
                    continue
                keep.append(r)
            keep.append((p0, p1, lo, hi, op, isw))
            self.recs[name] = keep
        if is_dma:
            i = self.dcnt[eng] % len(self.dsem[eng])
            self.dcnt[eng] += 1
            op.sem = self.dsem[eng][i]
            prev = self.dlast.get(op.sem)
            if prev is not None:
                op.dmadeps.append(prev)
            self.dval[op.sem] = self.dval.get(op.sem, 0) + 16
            op.semval = self.dval[op.sem]
            self.dlast[op.sem] = op
        self.streams[eng].append(op)
        return op

    def finalize(self):
        for e, lst in self.streams.items():
            for op in lst:
                for E, idx in op.deps.items():
                    self.streams[E][idx].signal = True
        for e, lst in self.streams.items():
            k = 0
            for op in lst:
                if op.signal and not op.is_dma:
                    k += 1
                    op.ordinal = k
            nep = (k + EPOCH - 1) // EPOCH
            self.esem[e] = [self.es.enter_context(self.nc.semaphore(f"c{e}{i}")) for i in range(max(nep, 1))]

    def emit(self, engname, eng):
        waited = {}
        for op in self.streams[engname]:
            for E, idx in op.deps.items():
                k = self.streams[E][idx].ordinal
                if waited.get(E, 0) >= k:
                    continue
                eng.wait_ge(self.esem[E][(k - 1) // EPOCH], (k - 1) % EPOCH + 1)
                waited[E] = k
            for d in op.dmadeps:
                if waited.get(d.sem, 0) >= d.semval:
                    continue
                eng.wait_ge(d.sem, d.semval)
                waited[d.sem] = d.semval
            ins = op.fn(eng)
            if op.is_dma:
                ins.then_inc(op.sem, 16)
            elif op.signal:
                k = op.ordinal
                ins.then_inc(self.esem[engname][(k - 1) // EPOCH], 1)

    def mm(self, out, lhsT, rhs, start=True, stop=True):
        return self.add("pe", lambda e: e.matmul(out, lhsT=lhsT, rhs=rhs, start=start, stop=stop), [lhsT, rhs], [out])

    def tr(self, out, in_, ident):
        return self.add("pe", lambda e: e.transpose(out, in_, ident), [in_, ident], [out])

    def act(self, out, in_, func, bias=0.0, scale=1.0):
        return self.add("act", lambda e: e.activation(out, in_, func, bias=bias, scale=scale), [in_, bias, scale], [out])

    def tt(self, out, in0, in1, op, eng="dve"):
        return self.add(eng, lambda e: e.tensor_tensor(out, in0, in1, op), [in0, in1], [out])

    def ts(self, out, in0, s1, s2, op0, op1=None, eng="dve"):
        if op1 is None:
            return self.add(eng, lambda e: e.tensor_scalar(out, in0, s1, None, op0), [in0, s1], [out])
        return self.add(eng, lambda e: e.tensor_scalar(out, in0, s1, s2, op0, op1), [in0, s1, s2], [out])

    def stt(self, out, in0, scalar, in1, op0, op1, eng="dve"):
        return self.add(eng, lambda e: e.scalar_tensor_tensor(out, in0, scalar, in1, op0, op1), [in0, scalar, in1], [out])

    def copy(self, out, in_, eng=None):
        if eng is None:
            self.alt ^= 1
            eng = "act" if self.alt else "dve"
        if eng == "act":
            return self.add("act", lambda e: e.activation(out, in_, AF.Copy), [in_], [out])
        return self.add(eng, lambda e: e.tensor_copy(out, in_), [in_], [out])

    def memset(self, ap, val, eng="dve"):
        return self.add(eng, lambda e: e.memset(ap, val), [], [ap])

    def dma(self, q, out, in_):
        return self.add(q, lambda e: e.dma_start(out=out, in_=in_), [in_], [out], is_dma=True)


NPV = 16 * 3 + 12 + 48 + 8 + 8 + 96
PV_GMIX, PV_GFFN, PV_GFIN, PV_CB, PV_CW, PV_GATT, PV_GSSD, PV_BADA = 0, 16, 32, 48, 60, 108, 116, 124


def build_program(NT, PRE, dbg_names=()):
    nc = bass.Bass("TRN2", target_bir_lowering=False)
    NTOK = (PRE + NT) * T

    def din(name, shape, dt=F32):
        return nc.dram_tensor(name, list(shape), dt, kind="ExternalInput").ap()

    xc = din("xc", [NTOK, 2048])
    cvec = din("cvec", [128, 16])
    w_ada = din("w_ada", [2048, 12288])
    w_in = din("w_in", [2048, 5648])
    w_out = din("w_out", [2048, 2048])
    w_gate = din("w_gate", [2048, 5632])
    w_up = din("w_up", [2048, 5632])
    w_down = din("w_down", [5632, 2048])
    pvec_d = din("pvec", [128, NPV])
    bvec_d = din("bvec", [128, 48])
    flag_d = din("flag", [128, 2])
    biasT_d = din("biasT", [8, 128, 1280])
    c32_d = din("c32", [128, 5 * 128 + 64 + 1024])
    c16_d = din("c16", [128, 4 * 128 + 1024], BF16)
    y_d = nc.dram_tensor("y", [NT * T, 2048], F32, kind="ExternalOutput").ap()
    dbg_d = {}

    es = contextlib.ExitStack()
    with es:
        S = Sched(nc, es)

        def sb(name, shape, dt):
            return es.enter_context(nc.sbuf_tensor("s_" + name, list(shape), dt))

        xT = sb("xT", [128, 16, 512], F32)
        Kb = [sb(f"K{i}", [128, 8, 512], BF16) for i in range(2)]
        Vb = [sb(f"V{i}", [128, 4, 1024], BF16) for i in range(2)]
        St = sb("St", [128, 1024], F32)
        Sbf = [sb(f"Sbf{i}", [128, 1024], BF16) for i in range(2)]
        halo = sb("halo", [128, 12, 3], F32)
        pv = sb("pv", [128, NPV], F32)
        bv = sb("bv", [128, 48], F32)
        flag = sb("flag", [128, 2], F32)
        c32 = sb("c32", [128, 5 * 128 + 64 + 1024], F32)
        c16 = sb("c16", [128, 4 * 128 + 1024], BF16)
        mods = sb("mods", [128, 96], F32)
        a12 = sb("a12", [128, 32], F32)
        a_b = sb("a_b", [128, 16], F32)
        cvs = sb("cvs", [128, 16], F32)
        condb = sb("condb", [128, 16], BF16)
        wdt = sb("wdt", [128, 16, 16], BF16)
        NW = 3
        wbuf = [sb(f"w{i}", [128, 4096], BF16) for i in range(NW)]
        RH = sb("RH", [128, 8192], BF16)
        RQ = sb("RQ", [128, 8192], BF16)
        RZ = sb("RZ", [128, 4096], BF16)
        RX = sb("RX", [128, 8192], BF16)
        RT = sb("RT", [128, 5120], BF16)
        RS = sb("RS", [128, 8576], F32)
        psum = es.enter_context(nc.psum_tensor("ps", [128, 8, 512], F32))

        ident32 = c32[:, 0:128]
        blockones = c32[:, 128:256]
        BT2 = c32[:, 256:384]
        selA = c32[:, 384:512]
        selB = c32[:, 512:640]
        tri2 = c32[:, 640:704]
        negmask = c32[:, 704:1728]
        ident16 = c16[:, 0:128]
        ones16 = c16[:, 128:256]
        onesD = c16[:, 256:384]
        onesH = c16[:, 384:512]
        cmask = c16[:, 512:1536]
        gmix, gffn, gfin = pv[:, 0:16], pv[:, 16:32], pv[:, 32:48]
        convb = pv[:, PV_CB:PV_CB + 12]
        convw = pv[:, PV_CW:PV_CW + 48]
        gatt, gssd = pv[:, PV_GATT:PV_GATT + 8], pv[:, PV_GSSD:PV_GSSD + 8]
        bada = pv[:, PV_BADA:PV_BADA + 96]
        dtb, alog, dskip = bv[:, 0:16], bv[:, 16:32], bv[:, 32:48]
        sh1, sc1, gt1 = mods[:, 0:16], mods[:, 16:32], mods[:, 32:48]
        sh2, sc2, gt2 = mods[:, 48:64], mods[:, 64:80], mods[:, 80:96]
        a1, a2 = a12[:, 0:16], a12[:, 16:32]

        hT = RH[:, :].rearrange("p (a b) -> p a b", a=16)
        attT = RH[:, 0:4096].rearrange("p (a b) -> p a b", a=8)
        gyT = RH[:, 4096:8192].rearrange("p (a b) -> p a b", a=8)
        qbd = RQ[:, :].rearrange("p (h j c) -> p h j c", h=8, j=8)
        mixT = RQ[:, :].rearrange("p (a b) -> p a b", a=16)
        szT = RZ[:, :].rearrange("p (a b) -> p a b", a=8)
        actT = [RZ[:, i * 2048:(i + 1) * 2048].rearrange("p (a b) -> p a b", a=4) for i in range(2)]
        xsT = RX[:, 0:6144].rearrange("p (a b) -> p a b", a=12)
        CTpad = RX[:, 6144:8192].rearrange("p (c g t) -> p c g t", c=2, g=2)
        RXf = RX[:, :].bitcast(F32)
        sgt = [RXf[:, i * 512:(i + 1) * 512] for i in range(4)]
        xtok = RT[:, 0:4096].rearrange("p (a b) -> p a b", a=4)
        Btok = RT[:, 4096:5120].rearrange("p (a b) -> p a b", a=4)
        TA = RS[:, 0:1024]
        TB = RS[:, 1024:2048]
        TC = RS[:, 2048:3072]
        xst = [TA, TB, TC]
        RSb = RS[:, :].bitcast(BF16)
        Mbd = RSb[:, 6144:8192].rearrange("p (r c) -> p r c", r=16)
        seg = RSb[:, 8192:9216]
        xdt = RSb[:, 9216:10240]
        xdtw = RSb[:, 10240:11264]
        rstd = RS[:, 5632:6144]
        rstd2 = RS[:, 6144:6656]
        small = RS[:, 6656:7168]
        cbsb = RS[:, 7168:7296].rearrange("p (g l) -> p g l", g=2)
        sqr = [RSb[:, 14592 + i * 512:14592 + (i + 1) * 512] for i in range(3)]
        tmpf = [RS[:, 8064:8576], None]
        pTt = [RSb[:, 11264 + k * 640:11264 + (k + 1) * 640] for k in range(3)]
        pTt.append(RSb[:, 4608:5248])
        recf = [RS[:, 2048:2176], RS[:, 2176:2304]]
        sbt = [RXf[:, k * 640:(k + 1) * 640] for k in range(3)]
        biasb = [RS[:, 7296:7936], RS[:, 7936:8576]]
        xbcr = [RS[:, 0:515], RS[:, 515:1030]]
        cacc = [RS[:, 1030:1542], RS[:, 1542:2054]]
        tmpf[1] = RS[:, 3072:3584]

        bankctr = [0]

        def bank():
            i = bankctr[0] % 6
            bankctr[0] += 1
            return psum[:, i, :]

        wctr = [0]

        def wload(src, shape3):
            b = wbuf[wctr[0] % NW]
            wctr[0] += 1
            a_, b_ = shape3
            view = b[:, 0:a_ * b_].rearrange("p (a b) -> p a b", a=a_)
            S.dma("pool", view, src)
            return view

        def dbg(name, ap, shape):
            if name in dbg_names:
                d = nc.dram_tensor("dbg_" + name, list(shape), ap.dtype, kind="ExternalOutput").ap()
                dbg_d[name] = d
                S.dma("sp", d, ap)

        S.dma("sp", pv[:], pvec_d)
        S.dma("sp", bv[:], bvec_d)
        S.dma("sp", flag[:], flag_d)
        S.dma("sp", c32[:], c32_d)
        S.dma("sp", c16[:], c16_d)
        S.dma("sp", cvs[:], cvec)
        S.dma("pool", wdt[:], w_in[:, 5632:5648].rearrange("(kc p) n -> p kc n", p=128))
        S.memset(halo[:], 0.0)
        S.memset(St[:], 0.0)
        for i in range(2):
            S.memset(Kb[i][:], 0.0)
            S.memset(Vb[i][:], 0.0)
        S.act(condb[:], cvs[:], AF.Silu)
        mps = bank()
        wav = w_ada.rearrange("(kc p) n -> p kc n", p=128)
        for u in range(48):
            wb = wload(wav[:, :, u * 256:(u + 1) * 256], (16, 256))
            for c2 in range(2):
                col = u * 2 + c2
                for kc in range(16):
                    S.mm(mps[:, col:col + 1], wb[:, kc, c2 * 128:(c2 + 1) * 128], condb[:, kc:kc + 1],
                         start=(kc == 0), stop=(kc == 15))
        S.tt(mods[:], mps[:, 0:96], bada, ALU.add)
        S.stt(a1, sc1, 1.0, gmix, ALU.add, ALU.mult)
        S.stt(a2, sc2, 1.0, gffn, ALU.add, ALU.mult)
        S.act(a_b[:], alog, AF.Exp)
        S.ts(a_b[:], a_b[:], -1.0, None, ALU.mult)
        dbg("mods", mods[:], [128, 96])

        def stats(src3, nk, ones_ap, out_rstd):
            sp_ = bank()
            for kc in range(nk):
                sq = sqr[kc % 3]
                S.act(sq, src3[:, kc, :], AF.Square)
                S.mm(sp_, ones_ap, sq, start=(kc == 0), stop=(kc == nk - 1))
            S.act(out_rstd, sp_, AF.Ln, bias=EPS)
            S.act(out_rstd, out_rstd, AF.Exp, scale=-0.5)

        def load_x(t0):
            for _ in load_x_gen(t0):
                pass

        def load_x_gen(t0):
            for blk in range(4):
                for half in range(2):
                    st = xst[(blk * 2 + half) % 3]
                    S.dma("sp", st, xc[t0 + blk * 128:t0 + (blk + 1) * 128, half * 1024:(half + 1) * 1024])
                    for g in range(2):
                        pb = bank()
                        for i in range(4):
                            S.tr(pb[:, i * 128:(i + 1) * 128], st[:, (g * 4 + i) * 128:(g * 4 + i + 1) * 128], ident32)
                        kc0 = half * 8 + g * 4
                        S.copy(xT[:, kc0:kc0 + 4, blk * 128:(blk + 1) * 128], pb.rearrange("p (a b) -> p a b", a=4))
                        yield

        def front_gen(ti):
            yield from load_x_gen(ti * T)
            sp_ = bank()
            for kc in range(16):
                sq = sqr[kc % 3]
                S.act(sq, xT[:, kc, :], AF.Square)
                S.mm(sp_, onesD, sq, start=(kc == 0), stop=(kc == 15))
                if kc % 4 == 3:
                    yield
            S.act(rstd, sp_, AF.Ln, bias=EPS)
            S.act(rstd, rstd, AF.Exp, scale=-0.5)
            yield
            for kc in range(16):
                tm = tmpf[kc % 2]
                S.stt(tm, xT[:, kc, :], a1[:, kc:kc + 1], rstd, ALU.mult, ALU.mult)
                S.act(hT[:, kc, :], tm, AF.Identity, bias=sh1[:, kc:kc + 1])
                if kc % 2 == 1:
                    yield

        def make_h(rs, a_, sh_):
            for kc in range(16):
                tm = tmpf[kc % 2]
                S.stt(tm, xT[:, kc, :], a_[:, kc:kc + 1], rs, ALU.mult, ALU.mult)
                S.act(hT[:, kc, :], tm, AF.Identity, bias=sh_[:, kc:kc + 1])

        winv = w_in.rearrange("(kc p) n -> p kc n", p=128)

        def inproj(ti, mode):
            cur = ti % 2
            if mode == "main":
                S.memset(RQ[:, :], 0.0)
            units = list(range(22)) if mode == "main" else (
                list(range(16, 21)) if mode == "pre" else list(range(4, 12)) + list(range(16, 22)))
            for u in units:
                wb = wload(winv[:, :, u * 256:(u + 1) * 256], (16, 256))
                if 8 <= u < 12:
                    for blk in range(4):
                        pb = bank()
                        for kc in range(16):
                            S.mm(pb[:, 0:256], hT[:, kc, blk * 128:(blk + 1) * 128], wb[:, kc, :],
                                 start=(kc == 0), stop=(kc == 15))
                        S.copy(Vb[cur][:, blk, (u - 8) * 256:(u - 7) * 256], pb[:, 0:256])
                    continue
                for c2 in range(2):
                    cc = u * 2 + c2
                    pb = bank()
                    for kc in range(16):
                        S.mm(pb, wb[:, kc, c2 * 128:(c2 + 1) * 128], hT[:, kc, :], start=(kc == 0), stop=(kc == 15))
                    if cc < 8:
                        hp = cc
                        S.ts(qbd[0:64, hp, :, 0:64], pb[0:64, :].rearrange("p (j q) -> p j q", j=8), 0.125, None, ALU.mult)
                        S.ts(qbd[64:128, hp, :, 64:128], pb[64:128, :].rearrange("p (j q) -> p j q", j=8), 0.125, None,
                             ALU.mult)
                    elif cc < 16:
                        S.copy(Kb[cur][:, cc - 8, :], pb)
                    elif cc < 32:
                        S.act(szT[:, cc - 24, :], pb, AF.Silu)
                    else:
                        xi = cc - 32
                        raw = xbcr[xi % 2]
                        acc = cacc[xi % 2]
                        S.copy(raw[:, 3:515], pb, eng="act")
                        S.copy(raw[:, 0:3], halo[:, xi, :], eng="dve")
                        S.ts(acc, raw[:, 0:512], convw[:, xi * 4:xi * 4 + 1], convb[:, xi:xi + 1], ALU.mult, ALU.add)
                        for j in range(1, 4):
                            S.stt(acc, raw[:, j:j + 512], convw[:, xi * 4 + j:xi * 4 + j + 1], acc, ALU.mult, ALU.add)
                        S.copy(halo[:, xi, :], raw[:, 512:515], eng="dve")
                        S.act(xsT[:, xi, :], acc, AF.Silu)
            dps = bank()
            for blk in range(4):
                for kc in range(16):
                    S.mm(dps[:, blk * 16:(blk + 1) * 16], hT[:, kc, blk * 128:(blk + 1) * 128], wdt[:, kc, :],
                         start=(kc == 0), stop=(kc == 15))
            dsb = small[:, 208:272]
            S.copy(dsb, dps[:, 0:64], eng="dve")
            return dsb

        def bc(ap2, n):
            return ap2.unsqueeze(2).broadcast_to([ap2.shape[0], ap2.shape[1], n])

        sbctr = [0]

        def sbank():
            sbctr[0] += 1
            return psum[:, 6 + sbctr[0] % 2, :]

        def sbank2():
            return psum[:, 6:8, :].rearrange("p a b -> p (a b)")

        def ssd_gen(ti, mode, dps):
            main = mode == "main"
            if main:
                S.memset(Mbd, 0.0, eng="pool")
            for blk in range(4):
                pb = sbank().bitcast(BF16)
                for fc in range(8):
                    S.tr(pb[:, fc * 128:(fc + 1) * 128], xsT[:, fc, blk * 128:(blk + 1) * 128], ident16)
                S.copy(xtok[:, blk, :], pb)
                pb2 = sbank().bitcast(BF16)
                for g in range(2):
                    S.tr(pb2[:, g * 128:(g + 1) * 128], xsT[:, 8 + g, blk * 128:(blk + 1) * 128], ident16)
                S.copy(Btok[:, blk, :], pb2[:, 0:256])
                yield
            if main:
                for c in range(2):
                    S.tt(CTpad[:, c, :, :], xsT[:, 10:12, :],
                         cmask[:, c * 512:(c + 1) * 512].unsqueeze(1).broadcast_to([128, 2, 512]), ALU.mult, eng="pool")
            dtv, ev, dt_, adt = small[:, 0:16], small[:, 16:32], small[:, 32:48], small[:, 48:64]
            cs4 = small[:, 64:128]
            dl, dec, cdAB, ecs = small[:, 128:144], small[:, 144:160], small[:, 160:192], small[:, 192:208]
            St3 = St[:, :].rearrange("p (r c) -> p r c", r=16)
            r16 = lambda ap: ap.rearrange("p (r c) -> p r c", r=16)
            for blk in range(4):
                tk = slice(blk * 128, (blk + 1) * 128)
                S.tt(dtv, dps[:, blk * 16:(blk + 1) * 16], dtb, ALU.add)
                S.act(ev, dtv, AF.Exp)
                S.act(dt_, ev, AF.Ln, bias=1.0)
                S.tt(adt, dt_, a_b[:], ALU.mult)
                yield
                cp = sbank()
                S.mm(cp[:, 0:16], BT2, adt)
                S.mm(cp[:, 16:32], blockones, adt)
                S.mm(cp[:, 32:48], selA, adt)
                S.mm(cp[:, 48:64], selB, adt)
                S.copy(cs4, cp[:, 0:64], eng="dve")
                S.tt(dl, cs4[:, 16:32], cs4[:, 0:16], ALU.subtract)
                S.act(dec, dl, AF.Exp)
                S.act(cdAB, cs4[:, 32:64], AF.Exp)
                if main:
                    S.act(ecs, cs4[:, 0:16], AF.Exp)
                yield
                xtok3 = r16(xtok[:, blk, :])
                S.tt(r16(xdt), xtok3, bc(dt_, 64), ALU.mult, eng="pool")
                S.tt(r16(xdtw), r16(xdt), bc(dec, 64), ALU.mult, eng="pool")
                yield
                if main:
                    cbp = sbank()
                    for g in range(2):
                        S.mm(cbp[:, g * 128:(g + 1) * 128], xsT[:, 8 + g, tk], xsT[:, 10 + g, tk])
                    cb3 = cbp[:, 0:256].rearrange("p (g l) -> p g l", g=2)
                    S.copy(cbsb[0:64, :, :], cb3[0:64, :, 0:64], eng="dve")
                    S.copy(cbsb[64:128, :, :], cb3[64:128, :, 64:128], eng="dve")
                    rhsD = TA
                    S.tt(r16(rhsD), tri2.unsqueeze(1).broadcast_to([128, 16, 64]), bc(adt, 64), ALU.mult, eng="pool")
                    yield
                    dp = sbank2()
                    for h2 in range(2):
                        S.mm(dp[:, h2 * 512:(h2 + 1) * 512], blockones, rhsD[:, h2 * 512:(h2 + 1) * 512], True, False)
                        S.mm(dp[:, h2 * 512:(h2 + 1) * 512], ident32, negmask[:, h2 * 512:(h2 + 1) * 512], False, True)
                    darg = TB
                    S.tt(r16(darg), r16(dp), bc(cs4[:, 0:16], 64), ALU.subtract)
                    S.act(seg, darg, AF.Exp)
                    yield
                    seg4 = seg.rearrange("p (g r c) -> p g r c", g=2, r=8)
                    Mbd4 = Mbd.rearrange("p (g r) c -> p g r c", g=2)
                    for c in range(2):
                        pr = slice(c * 64, (c + 1) * 64)
                        S.tt(Mbd4[pr, :, :, c * 64:(c + 1) * 64], seg4[pr],
                             cbsb[pr, :, :].unsqueeze(2).broadcast_to([64, 2, 8, 64]), ALU.mult)
                    yp = sbank2()
                    for r in range(16):
                        S.mm(yp[:, r * 64:(r + 1) * 64], Mbd[:, r, :], xdt[:, r * 64:(r + 1) * 64])
                    t1 = TA
                    S.copy(t1, yp, eng="act")
                    S.copy(Sbf[0][:], St[:], eng="act")
                    yield
                spA = sbank2()
                for g in range(2):
                    S.mm(spA[:, g * 512:(g + 1) * 512], Btok[0:64, blk, g * 128:(g + 1) * 128], xdtw[0:64, g * 512:(g + 1) * 512])
                S.tt(St3, St3, bc(cdAB[:, 0:16], 64), ALU.mult, eng="pool")
                S.tt(St[:], St[:], spA, ALU.add)
                yield
                if main:
                    S.copy(Sbf[1][:], St[:], eng="act")
                    op_ = sbank2()
                    for g in range(2):
                        S.mm(op_[:, g * 512:(g + 1) * 512], CTpad[:, 0, g, tk], Sbf[0][:, g * 512:(g + 1) * 512], True, False)
                        S.mm(op_[:, g * 512:(g + 1) * 512], CTpad[:, 1, g, tk], Sbf[1][:, g * 512:(g + 1) * 512], False, True)
                    t2 = TB
                    S.tt(r16(t2), r16(op_), bc(ecs, 64), ALU.mult)
                    S.tt(t1, t1, t2, ALU.add)
                    yield
                spB = sbank2()
                for g in range(2):
                    S.mm(spB[:, g * 512:(g + 1) * 512], Btok[64:128, blk, g * 128:(g + 1) * 128],
                         xdtw[64:128, g * 512:(g + 1) * 512])
                S.tt(St3, St3, bc(cdAB[:, 16:32], 64), ALU.mult, eng="pool")
                S.tt(St[:], St[:], spB, ALU.add)
                yield
                if main:
                    S.tt(r16(t2), xtok3, bc(dskip, 64), ALU.mult, eng="pool")
                    S.tt(t1, t1, t2, ALU.add, eng="pool")
                    if blk == 0 and ti == 0:
                        dbg("ytok", t1, [128, 1024])
                    tp = sbank2()
                    for fc in range(8):
                        S.tr(tp[:, fc * 128:(fc + 1) * 128], t1[:, fc * 128:(fc + 1) * 128], ident32)
                    S.tt(gyT[:, :, tk], tp.rearrange("p (a b) -> p a b", a=8), szT[:, :, tk], ALU.mult)
                    yield

        def attention(ti, first_main, steps):
            cur = ti % 2
            prev = 1 - cur
            its = [(hp, par, j) for hp in range(8) for par in range(2) for j in range(par, 8, 2)]
            N = len(its)

            def rows_of(par, m):
                if par == 0 and m == 4:
                    return slice(0, 64)
                if par == 1 and m == 0:
                    return slice(64, 128)
                return slice(0, 128)

            def stageA(i):
                hp, par, j = its[i]
                grp = i // 4
                bb = biasb[grp % 2]
                if i % 4 == 0:
                    S.dma("sp", bb, biasT_d[hp][:, par * 640:(par + 1) * 640])
                b0 = j // 2
                sps = psum[:, 2 * (i % 2):2 * (i % 2) + 2, :].rearrange("p a b -> p (a b)")
                for m in range(5):
                    b = b0 + m
                    kb = Kb[prev] if b < 4 else Kb[cur]
                    S.mm(sps[:, m * 128:(m + 1) * 128], kb[:, hp, (b % 4) * 128:(b % 4 + 1) * 128], qbd[:, hp, j, :])
                sbb = sbt[i % 3]
                S.tt(sbb, sps[:, 0:640], bb, ALU.add)
                pT = pTt[i % 4]
                npv = 4 - b0
                if first_main:
                    S.act(pT[:, 0:npv * 128], sbb[:, 0:npv * 128], AF.Exp, bias=flag[:, 1:2])
                    S.act(pT[:, npv * 128:640], sbb[:, npv * 128:640], AF.Exp)
                else:
                    S.act(pT, sbb, AF.Exp)

            def stageB(i):
                hp, par, j = its[i]
                b0 = j // 2
                pT = pTt[i % 4]
                ob = psum[:, 4 + i % 2, :]
                for m in range(5):
                    b = b0 + m
                    vb = Vb[prev] if b < 4 else Vb[cur]
                    rows = rows_of(par, m)
                    S.mm(ob[:, 0:128], vb[rows, b % 4, hp * 128:(hp + 1) * 128], pT[rows, m * 128:(m + 1) * 128],
                         start=(m == 0), stop=(m == 4))
                for m in range(5):
                    rows = rows_of(par, m)
                    S.mm(ob[:, 128:256], ones16[rows, :], pT[rows, m * 128:(m + 1) * 128], start=(m == 0), stop=(m == 4))
                rc = recf[i % 2]
                S.act(rc, ob[:, 128:256], AF.Ln)
                S.act(rc, rc, AF.Exp, scale=-1.0)
                S.tt(attT[0:64, hp, j * 64:(j + 1) * 64], ob[0:64, 0:64], rc[0:64, 0:64], ALU.mult)
                S.tt(attT[64:128, hp, j * 64:(j + 1) * 64], ob[64:128, 64:128], rc[64:128, 64:128], ALU.mult)

            LAG = 3
            for i in range(N + LAG):
                if i < N:
                    stageA(i)
                if i >= LAG:
                    stageB(i - LAG)
                next(steps, None)
            for _ in steps:
                pass

        def mixer_out():
            stats(attT, 8, onesH, rstd)
            stats(gyT, 8, onesH, rstd2)
            for fc in range(8):
                S.stt(mixT[:, fc, :], attT[:, fc, :], gatt[:, fc:fc + 1], rstd, ALU.mult, ALU.mult)
            for fc in range(8):
                S.stt(mixT[:, 8 + fc, :], gyT[:, fc, :], gssd[:, fc:fc + 1], rstd2, ALU.mult, ALU.mult)
            wov = w_out.rearrange("(kc p) n -> p kc n", p=128)
            for u in range(8):
                wb = wload(wov[:, :, u * 256:(u + 1) * 256], (16, 256))
                for c2 in range(2):
                    c = u * 2 + c2
                    pb = bank()
                    for kc in range(16):
                        S.mm(pb, wb[:, kc, c2 * 128:(c2 + 1) * 128], mixT[:, kc, :], start=(kc == 0), stop=(kc == 15))
                    S.stt(xT[:, c, :], pb, gt1[:, c:c + 1], xT[:, c, :], ALU.mult, ALU.add)

        wgv = w_gate.rearrange("(kc p) n -> p kc n", p=128)
        wuv = w_up.rearrange("(kc p) n -> p kc n", p=128)

        def ffn():
            stats(xT, 16, onesD, rstd)
            make_h(rstd, a2, sh2)
            for g in range(11):
                at = actT[g % 2]
                for half in range(2):
                    u = g * 2 + half
                    wg = wload(wgv[:, :, u * 256:(u + 1) * 256], (16, 256))
                    wu = wload(wuv[:, :, u * 256:(u + 1) * 256], (16, 256))
                    for c2 in range(2):
                        gp = bank()
                        for kc in range(16):
                            S.mm(gp, wg[:, kc, c2 * 128:(c2 + 1) * 128], hT[:, kc, :], start=(kc == 0), stop=(kc == 15))
                        up = bank()
                        for kc in range(16):
                            S.mm(up, wu[:, kc, c2 * 128:(c2 + 1) * 128], hT[:, kc, :], start=(kc == 0), stop=(kc == 15))
                        sg = sgt[(half * 2 + c2) % 4]
                        S.act(sg, gp, AF.Silu)
                        S.tt(at[:, half * 2 + c2, :], sg, up, ALU.mult)
                for oh in range(2):
                    wd = wload(w_down[g * 512:(g + 1) * 512, oh * 1024:(oh + 1) * 1024].rearrange("(kc p) n -> p kc n", p=128),
                               (4, 1024))
                    for o8 in range(8):
                        oc = oh * 8 + o8
                        pb = bank()
                        for kc in range(4):
                            S.mm(pb, wd[:, kc, o8 * 128:(o8 + 1) * 128], at[:, kc, :], start=(kc == 0), stop=(kc == 3))
                        S.stt(xT[:, oc, :], pb, gt2[:, oc:oc + 1], xT[:, oc, :], ALU.mult, ALU.add)

        def final_out(tm):
            stats(xT, 16, onesD, rstd)
            for kc in range(16):
                S.stt(xT[:, kc, :], xT[:, kc, :], gfin[:, kc:kc + 1], rstd, ALU.mult, ALU.mult)
            for blk in range(4):
                for half in range(2):
                    st = xst[(blk * 2 + half) % 3]
                    for g in range(2):
                        pb = bank()
                        for i in range(4):
                            kc = half * 8 + g * 4 + i
                            S.tr(pb[:, i * 128:(i + 1) * 128], xT[:, kc, blk * 128:(blk + 1) * 128], ident32)
                        S.copy(st[:, g * 512:(g + 1) * 512], pb)
                    S.dma("sp", y_d[tm * T + blk * 128:tm * T + (blk + 1) * 128, half * 1024:(half + 1) * 1024], st)

        front_done = set()
        for ti in range(PRE + NT):
            mode = "main" if ti >= PRE else ("prelast" if ti == PRE - 1 else "pre")
            if ti not in front_done:
                for _ in front_gen(ti):
                    pass
            if ti == 0:
                dbg("hT", hT, [128, 16, 512])
            dps = inproj(ti, mode)
            if ti == 0:
                dbg("xsT", xsT, [128, 12, 512])
                dbg("szT", szT, [128, 8, 512])
                dbg("qbd", qbd, [128, 8, 8, 128])
                dbg("K", Kb[0][:], [128, 8, 512])
                dbg("V", Vb[0][:], [128, 4, 1024])
            steps = ssd_gen(ti, mode, dps)
            if mode != "main":
                nxt = front_gen(ti + 1)
                front_done.add(ti + 1)
                a_live, b_live = True, True
                while a_live or b_live:
                    if a_live:
                        a_live = next(steps, "END") != "END"
                    if b_live:
                        b_live = next(nxt, "END") != "END"
            if ti == PRE - 1:
                S.ts(St[:], St[:], flag[:, 0:1], None, ALU.mult)
                S.ts(halo[:], halo[:], flag[:, 0:1], None, ALU.mult)
            if mode == "main":
                for _ in range(4):
                    next(steps)
                attention(ti, ti == PRE, steps)
                if ti == PRE:
                    dbg("gyT", gyT, [128, 8, 512])
                if ti == PRE:
                    dbg("attT", attT, [128, 8, 512])
                mixer_out()
                if ti == PRE:
                    dbg("x1", xT[:], [128, 16, 512])
                ffn()
                final_out(ti - PRE)

        S.finalize()
        with nc.Block() as block:
            @block.tensor
            def _(e):
                S.emit("pe", e)

            @block.scalar
            def _(e):
                S.emit("act", e)

            @block.vector
            def _(e):
                S.emit("dve", e)

            @block.gpsimd
            def _(e):
                S.emit("pool", e)

            @block.sync
            def _(e):
                S.emit("sp", e)
                for sem, val in S.dval.items():
                    e.wait_ge(sem, val)
    return nc, list(dbg_d.keys())


def host_consts():
    import ml_dtypes
    c32 = np.zeros((128, 5 * 128 + 64 + 1024), np.float32)
    t = np.arange(128)
    c32[:, 0:128] = np.eye(128)
    same = (t[:, None] // 64) == (t[None, :] // 64)
    c32[:, 128:256] = same
    c32[:, 256:384] = same & (t[:, None] <= t[None, :])
    c32[:, 384:512] = (t[:, None] < 64) * np.ones((1, 128))
    c32[:, 512:640] = (t[:, None] >= 64) * np.ones((1, 128))
    l = np.arange(64)
    c32[:, 640:704] = (t[:, None] % 64) <= l[None, :]
    nm = np.where(l[None, :] < (t[:, None] % 64), NEG, 0.0)
    c32[:, 704:1728] = np.tile(nm, (1, 16))
    c16 = np.zeros((128, 4 * 128 + 1024), np.float32)
    c16[:, 0:128] = np.eye(128)
    c16[:, 128:256] = 1.0
    c16[:, 256:384] = 1.0 / 2048
    c16[:, 384:512] = 1.0 / 1024
    tok = np.arange(512)
    c16[:, 512:1024] = ((tok // 64) % 2 == 0)[None, :]
    c16[:, 1024:1536] = ((tok // 64) % 2 == 1)[None, :]
    return c32, c16.astype(ml_dtypes.bfloat16)


def host_bias(rel_bias):
    rb = np.asarray(rel_bias, np.float32)
    p = np.arange(128)[:, None, None, None]
    par = np.arange(2)[None, :, None, None]
    m = np.arange(5)[None, None, :, None]
    q = np.arange(64)[None, None, None, :]
    rel = 512 + 64 * par + q - 128 * m - p
    idx = np.clip(rel, -63, 256) + 63
    out = np.zeros((8, 128, 2, 5, 2, 64), np.float32)
    for hp in range(8):
        for hh in range(2):
            out[hp, :, :, :, hh, :] = rb[2 * hp + hh][idx]
    return out.reshape(8, 128, 1280)


def pcol(v, n):
    return np.ascontiguousarray(np.asarray(v, np.float32).reshape(n, 128).T)


def make_inmaps(inputs, NT, PRE, cores):
    c32, c16 = host_consts()
    biasT = host_bias(inputs["rel_bias"][0])
    pvec = np.zeros((128, NPV), np.float32)
    pvec[:, 0:16] = pcol(inputs["g_mix"][0], 16)
    pvec[:, 16:32] = pcol(inputs["g_ffn"][0], 16)
    pvec[:, 32:48] = pcol(inputs["g_final"], 16)
    pvec[:, PV_CB:PV_CB + 12] = pcol(inputs["conv_b"][0], 12)
    cw = np.asarray(inputs["conv_w"][0], np.float32)
    for xi in range(12):
        for j in range(4):
            pvec[:, PV_CW + xi * 4 + j] = cw[j, xi * 128:(xi + 1) * 128]
    pvec[:, PV_GATT:PV_GATT + 8] = pcol(inputs["g_att_out"][0], 8)
    pvec[:, PV_GSSD:PV_GSSD + 8] = pcol(inputs["g_ssd_out"][0], 8)
    pvec[:, PV_BADA:PV_BADA + 96] = pcol(inputs["b_ada"][0], 96)
    bvec = np.zeros((128, 48), np.float32)
    bvec[:, 0:16] = np.asarray(inputs["dt_bias"][0], np.float32)[None, :]
    bvec[:, 16:32] = np.asarray(inputs["a_log"][0], np.float32)[None, :]
    bvec[:, 32:48] = np.asarray(inputs["d_skip"][0], np.float32)[None, :]
    shared = dict(
        w_ada=np.ascontiguousarray(inputs["w_ada"][0]), w_in=np.ascontiguousarray(inputs["w_in"][0]),
        w_out=np.ascontiguousarray(inputs["w_out"][0]), w_gate=np.ascontiguousarray(inputs["w_gate"][0]),
        w_up=np.ascontiguousarray(inputs["w_up"][0]), w_down=np.ascontiguousarray(inputs["w_down"][0]),
        pvec=pvec, bvec=bvec, biasT=biasT, c32=c32, c16=c16)
    x = inputs["x"]
    maps = []
    for (b, t0) in cores:
        xcat = np.zeros(((PRE + NT) * T, 2048), np.float32)
        has_prev = t0 > 0
        if has_prev and PRE > 0:
            xcat[0:PRE * T] = x[b, t0 - PRE * T:t0]
        xcat[PRE * T:] = x[b, t0:t0 + NT * T]
        flag = np.zeros((128, 2), np.float32)
        flag[:, 0] = 1.0 if has_prev else 0.0
        flag[:, 1] = 0.0 if has_prev else NEG
        m = dict(shared)
        m.update(xc=xcat, cvec=pcol(inputs["c"][b], 16), flag=flag)
        maps.append(m)
    return maps


_CACHE = {}


def kernel(**inputs):
    inputs = {k: np.asarray(v) for k, v in inputs.items()}
    NT, PRE = 8, 8
    cores = [(b, h * 4096) for b in range(4) for h in range(2)]
    if "nc" not in _CACHE:
        _CACHE["nc"] = build_program(NT, PRE)[0]
    nc = _CACHE["nc"]
    maps = make_inmaps(inputs, NT, PRE, cores)
    res = run_bass_kernel_spmd(nc, maps, core_ids=list(range(8)))
    out = np.zeros((4, 8192, 2048), np.float32)
    for i, (b, t0) in enumerate(cores):
        out[b, t0:t0 + NT * T] = res.results[i]["y"]
    return out
```

```python
import contextlib
import numpy as np
import concourse.bass as bass
import concourse.mybir as mybir
from concourse.bass_utils import run_bass_kernel_spmd

F32, BF16 = mybir.dt.float32, mybir.dt.bfloat16
AF = mybir.ActivationFunctionType
ALU = mybir.AluOpType
ESZ = {F32: 4, BF16: 2}
EPS = 1e-6
T = 512
NEG = -30000.0
EPOCH = 30000
TRACKED = ("SBTensorHandle", "PSumTensorHandle")


class Op:
    __slots__ = ("eng", "fn", "deps", "dmadeps", "idx", "signal", "is_dma", "sem", "semval", "ordinal")


class Sched:
    def __init__(self, nc, es, nds=12):
        self.nc = nc
        self.streams = {"pe": [], "act": [], "dve": [], "pool": [], "sp": []}
        self.recs = {}
        self.dsem = {q: [es.enter_context(nc.semaphore(f"d{q}{i}")) for i in range(nds)] for q in ("pool", "sp")}
        self.dcnt = {"pool": 0, "sp": 0}
        self.dlast = {}
        self.dval = {}
        self.es = es
        self.esem = {}
        self.alt = 0

    @staticmethod
    def rect(ap):
        pairs = ap.ap
        off = ap.offset
        ps, pc = pairs[0]
        if ps == 0:
            ps = 1 << 40
        p0 = off // ps
        f0 = off % ps
        ext = 0
        for s, c in pairs[1:]:
            ext += (c - 1) * abs(s)
        e = ESZ[ap.dtype]
        return ap.tensor.name, p0, p0 + pc, f0 * e, (f0 + ext + 1) * e

    def add(self, eng, fn, reads, writes, is_dma=False):
        op = Op()
        op.eng, op.fn, op.deps, op.dmadeps, op.signal, op.is_dma = eng, fn, {}, [], False, is_dma
        op.idx = len(self.streams[eng])
        op.ordinal = 0
        for ap, isw in [(a, False) for a in reads] + [(a, True) for a in writes]:
            if ap is None or isinstance(ap, (int, float)):
                continue
            if type(ap.tensor).__name__ not in TRACKED:
                continue
            name, p0, p1, lo, hi = self.rect(ap)
            lst = self.recs.setdefault(name, [])
            keep = []
            for r in lst:
                rp0, rp1, rlo, rhi, rop, rw = r
                ov = rp0 < p1 and p0 < rp1 and rlo < hi and lo < rhi
                if ov and (rw or isw) and rop is not op:
                    if rop.is_dma:
                        if rop not in op.dmadeps:
                            op.dmadeps.append(rop)
                    elif not (rop.eng == "pe" and eng == "pe"):
                        if op.deps.get(rop.eng, -1) < rop.idx:
                            op.deps[rop.eng] = rop.idx
                cov = p0 <= rp0 and rp1 <= p1 and lo <= rlo and rhi <= hi
                if cov and (isw or (not rw and rop.eng == eng and not rop.is_dma and not is_dma)):
                    continue
                keep.append(r)
            keep.append((p0, p1, lo, hi, op, isw))
            self.recs[name] = keep
        if is_dma:
            i = self.dcnt[eng] % len(self.dsem[eng])
            self.dcnt[eng] += 1
            op.sem = self.dsem[eng][i]
            prev = self.dlast.get(op.sem)
            if prev is not None:
                op.dmadeps.append(prev)
            self.dval[op.sem] = self.dval.get(op.sem, 0) + 16
            op.semval = self.dval[op.sem]
            self.dlast[op.sem] = op
        self.streams[eng].append(op)
        return op

    def finalize(self):
        for e, lst in self.streams.items():
            for op in lst:
                for E, idx in op.deps.items():
                    self.streams[E][idx].signal = True
        for e, lst in self.streams.items():
            k = 0
            for op in lst:
                if op.signal and not op.is_dma:
                    k += 1
                    op.ordinal = k
            nep = (k + EPOCH - 1) // EPOCH
            self.esem[e] = [self.es.enter_context(self.nc.semaphore(f"c{e}{i}")) for i in range(max(nep, 1))]

    def emit(self, engname, eng):
        waited = {}
        for op in self.streams[engname]:
            for E, idx in op.deps.items():
                k = self.streams[E][idx].ordinal
                if waited.get(E, 0) >= k:
                    continue
                eng.wait_ge(self.esem[E][(k - 1) // EPOCH], (k - 1) % EPOCH + 1)
                waited[E] = k
            for d in op.dmadeps:
                if waited.get(d.sem, 0) >= d.semval:
                    continue
                eng.wait_ge(d.sem, d.semval)
                waited[d.sem] = d.semval
            ins = op.fn(eng)
            if op.is_dma:
                ins.then_inc(op.sem, 16)
            elif op.signal:
                k = op.ordinal
                ins.then_inc(self.esem[engname][(k - 1) // EPOCH], 1)

    def mm(self, out, lhsT, rhs, start=True, stop=True):
        return self.add("pe", lambda e: e.matmul(out, lhsT=lhsT, rhs=rhs, start=start, stop=stop), [lhsT, rhs], [out])

    def tr(self, out, in_, ident):
        return self.add("pe", lambda e: e.transpose(out, in_, ident), [in_, ident], [out])

    def act(self, out, in_, func, bias=0.0, scale=1.0):
        return self.add("act", lambda e: e.activation(out, in_, func, bias=bias, scale=scale), [in_, bias, scale], [out])

    def tt(self, out, in0, in1, op, eng="dve"):
        return self.add(eng, lambda e: e.tensor_tensor(out, in0, in1, op), [in0, in1], [out])

    def ts(self, out, in0, s1, s2, op0, op1=None, eng="dve"):
        if op1 is None:
            return self.add(eng, lambda e: e.tensor_scalar(out, in0, s1, None, op0), [in0, s1], [out])
        return self.add(eng, lambda e: e.tensor_scalar(out, in0, s1, s2, op0, op1), [in0, s1, s2], [out])

    def stt(self, out, in0, scalar, in1, op0, op1, eng="dve"):
        return self.add(eng, lambda e: e.scalar_tensor_tensor(out, in0, scalar, in1, op0, op1), [in0, scalar, in1], [out])

    def copy(self, out, in_, eng=None):
        if eng is None:
            self.alt ^= 1
            eng = "act" if self.alt else "dve"
        if eng == "act":
            return self.add("act", lambda e: e.activation(out, in_, AF.Copy), [in_], [out])
        return self.add(eng, lambda e: e.tensor_copy(out, in_), [in_], [out])

    def memset(self, ap, val, eng="dve"):
        return self.add(eng, lambda e: e.memset(ap, val), [], [ap])

    def dma(self, q, out, in_):
        return self.add(q, lambda e: e.dma_start(out=out, in_=in_), [in_], [out], is_dma=True)


NPV = 16 * 3 + 12 + 48 + 8 + 8 + 96
PV_GMIX, PV_GFFN, PV_GFIN, PV_CB, PV_CW, PV_GATT, PV_GSSD, PV_BADA = 0, 16, 32, 48, 60, 108, 116, 124


def build_program(NT, PRE, dbg_names=()):
    nc = bass.Bass("TRN2", target_bir_lowering=False)
    NTOK = (PRE + NT) * T

    def din(name, shape, dt=F32):
        return nc.dram_tensor(name, list(shape), dt, kind="ExternalInput").ap()

    xc = din("xc", [NTOK, 2048])
    cvec = din("cvec", [128, 16])
    w_ada = din("w_ada", [2048, 12288])
    w_in = din("w_in", [2048, 5648])
    w_out = din("w_out", [2048, 2048])
    w_gate = din("w_gate", [2048, 5632])
    w_up = din("w_up", [2048, 5632])
    w_down = din("w_down", [5632, 2048])
    pvec_d = din("pvec", [128, NPV])
    bvec_d = din("bvec", [128, 48])
    flag_d = din("flag", [128, 2])
    biasT_d = din("biasT", [8, 128, 1280])
    c32_d = din("c32", [128, 5 * 128 + 64 + 1024])
    c16_d = din("c16", [128, 4 * 128 + 1024], BF16)
    y_d = nc.dram_tensor("y", [NT * T, 2048], F32, kind="ExternalOutput").ap()
    dbg_d = {}

    es = contextlib.ExitStack()
    with es:
        S = Sched(nc, es)

        def sb(name, shape, dt):
            return es.enter_context(nc.sbuf_tensor("s_" + name, list(shape), dt))

        xT = sb("xT", [128, 16, 512], F32)
        Kb = [sb(f"K{i}", [128, 8, 512], BF16) for i in range(2)]
        Vb = [sb(f"V{i}", [128, 4, 1024], BF16) for i in range(2)]
        St = sb("St", [128, 1024], F32)
        Sbf = [sb(f"Sbf{i}", [128, 1024], BF16) for i in range(2)]
        halo = sb("halo", [128, 12, 3], F32)
        pv = sb("pv", [128, NPV], F32)
        bv = sb("bv", [128, 48], F32)
        flag = sb("flag", [128, 2], F32)
        c32 = sb("c32", [128, 5 * 128 + 64 + 1024], F32)
        c16 = sb("c16", [128, 4 * 128 + 1024], BF16)
        mods = sb("mods", [128, 96], F32)
        a12 = sb("a12", [128, 32], F32)
        a_b = sb("a_b", [128, 16], F32)
        cvs = sb("cvs", [128, 16], F32)
        condb = sb("condb", [128, 16], BF16)
        wdt = sb("wdt", [128, 16, 16], BF16)
        NW = 3
        wbuf = [sb(f"w{i}", [128, 4096], BF16) for i in range(NW)]
        RH = sb("RH", [128, 8192], BF16)
        RQ = sb("RQ", [128, 8192], BF16)
        RZ = sb("RZ", [128, 4096], BF16)
        RX = sb("RX", [128, 8192], BF16)
        RT = sb("RT", [128, 5120], BF16)
        RS = sb("RS", [128, 8576], F32)
        psum = es.enter_context(nc.psum_tensor("ps", [128, 8, 512], F32))

        ident32 = c32[:, 0:128]
        blockones = c32[:, 128:256]
        BT2 = c32[:, 256:384]
        selA = c32[:, 384:512]
        selB = c32[:, 512:640]
        tri2 = c32[:, 640:704]
        negmask = c32[:, 704:1728]
        ident16 = c16[:, 0:128]
        ones16 = c16[:, 128:256]
        onesD = c16[:, 256:384]
        onesH = c16[:, 384:512]
        cmask = c16[:, 512:1536]
        gmix, gffn, gfin = pv[:, 0:16], pv[:, 16:32], pv[:, 32:48]
        convb = pv[:, PV_CB:PV_CB + 12]
        convw = pv[:, PV_CW:PV_CW + 48]
        gatt, gssd = pv[:, PV_GATT:PV_GATT + 8], pv[:, PV_GSSD:PV_GSSD + 8]
        bada = pv[:, PV_BADA:PV_BADA + 96]
        dtb, alog, dskip = bv[:, 0:16], bv[:, 16:32], bv[:, 32:48]
        sh1, sc1, gt1 = mods[:, 0:16], mods[:, 16:32], mods[:, 32:48]
        sh2, sc2, gt2 = mods[:, 48:64], mods[:, 64:80], mods[:, 80:96]
        a1, a2 = a12[:, 0:16], a12[:, 16:32]

        hT = RH[:, :].rearrange("p (a b) -> p a b", a=16)
        attT = RH[:, 0:4096].rearrange("p (a b) -> p a b", a=8)
        gyT = RH[:, 4096:8192].rearrange("p (a b) -> p a b", a=8)
        qbd = RQ[:, :].rearrange("p (h j c) -> p h j c", h=8, j=8)
        mixT = RQ[:, :].rearrange("p (a b) -> p a b", a=16)
        szT = RZ[:, :].rearrange("p (a b) -> p a b", a=8)
        actT = [RZ[:, i * 2048:(i + 1) * 2048].rearrange("p (a b) -> p a b", a=4) for i in range(2)]
        xsT = RX[:, 0:6144].rearrange("p (a b) -> p a b", a=12)
        CTpad = RX[:, 6144:8192].rearrange("p (c g t) -> p c g t", c=2, g=2)
        RXf = RX[:, :].bitcast(F32)
        sgt = [RXf[:, i * 512:(i + 1) * 512] for i in range(4)]
        xtok = RT[:, 0:4096].rearrange("p (a b) -> p a b", a=4)
        Btok = RT[:, 4096:5120].rearrange("p (a b) -> p a b", a=4)
        TA = RS[:, 0:1024]
        TB = RS[:, 1024:2048]
        TC = RS[:, 2048:3072]
        xst = [TA, TB, TC]
        RSb = RS[:, :].bitcast(BF16)
        Mbd = RSb[:, 6144:8192].rearrange("p (r c) -> p r c", r=16)
        seg = RSb[:, 8192:9216]
        xdt = RSb[:, 9216:10240]
        xdtw = RSb[:, 10240:11264]
        rstd = RS[:, 5632:6144]
        rstd2 = RS[:, 6144:6656]
        small = RS[:, 6656:7168]
        cbsb = RS[:, 7168:7296].rearrange("p (g l) -> p g l", g=2)
        sqr = [RSb[:, 14592 + i * 512:14592 + (i + 1) * 512] for i in range(3)]
        tmpf = [RS[:, 8064:8576], None]
        pTt = [RSb[:, 11264 + k * 640:11264 + (k + 1) * 640] for k in range(3)]
        pTt.append(RSb[:, 4608:5248])
        recf = [RS[:, 2048:2176], RS[:, 2176:2304]]
        sbt = [RXf[:, k * 640:(k + 1) * 640] for k in range(3)]
        biasb = [RS[:, 7296:7936], RS[:, 7936:8576]]
        xbcr = [RS[:, 0:515], RS[:, 515:1030]]
        cacc = [RS[:, 1030:1542], RS[:, 1542:2054]]
        tmpf[1] = RS[:, 3072:3584]

        bankctr = [0]

        def bank():
            i = bankctr[0] % 6
            bankctr[0] += 1
            return psum[:, i, :]

        wctr = [0]

        def wload(src, shape3):
            b = wbuf[wctr[0] % NW]
            wctr[0] += 1
            a_, b_ = shape3
            view = b[:, 0:a_ * b_].rearrange("p (a b) -> p a b", a=a_)
            S.dma("pool", view, src)
            return view

        def dbg(name, ap, shape):
            if name in dbg_names:
                d = nc.dram_tensor("dbg_" + name, list(shape), ap.dtype, kind="ExternalOutput").ap()
                dbg_d[name] = d
                S.dma("sp", d, ap)

        S.dma("sp", pv[:], pvec_d)
        S.dma("sp", bv[:], bvec_d)
        S.dma("sp", flag[:], flag_d)
        S.dma("sp", c32[:], c32_d)
        S.dma("sp", c16[:], c16_d)
        S.dma("sp", cvs[:], cvec)
        S.dma("pool", wdt[:], w_in[:, 5632:5648].rearrange("(kc p) n -> p kc n", p=128))
        S.memset(halo[:], 0.0)
        S.memset(St[:], 0.0)
        for i in range(2):
            S.memset(Kb[i][:], 0.0)
            S.memset(Vb[i][:], 0.0)
        S.act(condb[:], cvs[:], AF.Silu)
        mps = bank()
        wav = w_ada.rearrange("(kc p) n -> p kc n", p=128)
        for u in range(48):
            wb = wload(wav[:, :, u * 256:(u + 1) * 256], (16, 256))
            for c2 in range(2):
                col = u * 2 + c2
                for kc in range(16):
                    S.mm(mps[:, col:col + 1], wb[:, kc, c2 * 128:(c2 + 1) * 128], condb[:, kc:kc + 1],
                         start=(kc == 0), stop=(kc == 15))
        S.tt(mods[:], mps[:, 0:96], bada, ALU.add)
        S.stt(a1, sc1, 1.0, gmix, ALU.add, ALU.mult)
        S.stt(a2, sc2, 1.0, gffn, ALU.add, ALU.mult)
        S.act(a_b[:], alog, AF.Exp)
        S.ts(a_b[:], a_b[:], -1.0, None, ALU.mult)
        dbg("mods", mods[:], [128, 96])

        def stats(src3, nk, ones_ap, out_rstd):
            sp_ = bank()
            for kc in range(nk):
                sq = sqr[kc % 3]
                S.act(sq, src3[:, kc, :], AF.Square)
                S.mm(sp_, ones_ap, sq, start=(kc == 0), stop=(kc == nk - 1))
            S.act(out_rstd, sp_, AF.Ln, bias=EPS)
            S.act(out_rstd, out_rstd, AF.Exp, scale=-0.5)

        def load_x(t0):
            for _ in load_x_gen(t0):
                pass

        def load_x_gen(t0):
            for blk in range(4):
                for half in range(2):
                    st = xst[(blk * 2 + half) % 3]
                    S.dma("sp", st, xc[t0 + blk * 128:t0 + (blk + 1) * 128, half * 1024:(half + 1) * 1024])
                    for g in range(2):
                        pb = bank()
                        for i in range(4):
                            S.tr(pb[:, i * 128:(i + 1) * 128], st[:, (g * 4 + i) * 128:(g * 4 + i + 1) * 128], ident32)
                        kc0 = half * 8 + g * 4
                        S.copy(xT[:, kc0:kc0 + 4, blk * 128:(blk + 1) * 128], pb.rearrange("p (a b) -> p a b", a=4))
                        yield

        def front_gen(ti):
            yield from load_x_gen(ti * T)
            sp_ = bank()
            for kc in range(16):
                sq = sqr[kc % 3]
                S.act(sq, xT[:, kc, :], AF.Square)
                S.mm(sp_, onesD, sq, start=(kc == 0), stop=(kc == 15))
                if kc % 4 == 3:
                    yield
            S.act(rstd, sp_, AF.Ln, bias=EPS)
            S.act(rstd, rstd, AF.Exp, scale=-0.5)
            yield
            for kc in range(16):
                tm = tmpf[kc % 2]
                S.stt(tm, xT[:, kc, :], a1[:, kc:kc + 1], rstd, ALU.mult, ALU.mult)
                S.act(hT[:, kc, :], tm, AF.Identity, bias=sh1[:, kc:kc + 1])
                if kc % 2 == 1:
                    yield

        def make_h(rs, a_, sh_):
            for kc in range(16):
                tm = tmpf[kc % 2]
                S.stt(tm, xT[:, kc, :], a_[:, kc:kc + 1], rs, ALU.mult, ALU.mult)
                S.act(hT[:, kc, :], tm, AF.Identity, bias=sh_[:, kc:kc + 1])

        winv = w_in.rearrange("(kc p) n -> p kc n", p=128)

        def inproj(ti, mode):
            cur = ti % 2
            if mode == "main":
                S.memset(RQ[:, :], 0.0)
            units = list(range(22)) if mode == "main" else (
                list(range(16, 21)) if mode == "pre" else list(range(4, 12)) + list(range(16, 22)))
            for u in units:
                wb = wload(winv[:, :, u * 256:(u + 1) * 256], (16, 256))
                if 8 <= u < 12:
                    for blk in range(4):
                        pb = bank()
                        for kc in range(16):
                            S.mm(pb[:, 0:256], hT[:, kc, blk * 128:(blk + 1) * 128], wb[:, kc, :],
                                 start=(kc == 0), stop=(kc == 15))
                        S.copy(Vb[cur][:, blk, (u - 8) * 256:(u - 7) * 256], pb[:, 0:256])
                    continue
                for c2 in range(2):
                    cc = u * 2 + c2
                    pb = bank()
                    for kc in range(16):
                        S.mm(pb, wb[:, kc, c2 * 128:(c2 + 1) * 128], hT[:, kc, :], start=(kc == 0), stop=(kc == 15))
                    if cc < 8:
                        hp = cc
                        S.ts(qbd[0:64, hp, :, 0:64], pb[0:64, :].rearrange("p (j q) -> p j q", j=8), 0.125, None, ALU.mult)
                        S.ts(qbd[64:128, hp, :, 64:128], pb[64:128, :].rearrange("p (j q) -> p j q", j=8), 0.125, None,
                             ALU.mult)
                    elif cc < 16:
                        S.copy(Kb[cur][:, cc - 8, :], pb)
                    elif cc < 32:
                        S.act(szT[:, cc - 24, :], pb, AF.Silu)
                    else:
                        xi = cc - 32
                        raw = xbcr[xi % 2]
                        acc = cacc[xi % 2]
                        S.copy(raw[:, 3:515], pb, eng="act")
                        S.copy(raw[:, 0:3], halo[:, xi, :], eng="dve")
                        S.ts(acc, raw[:, 0:512], convw[:, xi * 4:xi * 4 + 1], convb[:, xi:xi + 1], ALU.mult, ALU.add)
                        for j in range(1, 4):
                            S.stt(acc, raw[:, j:j + 512], convw[:, xi * 4 + j:xi * 4 + j + 1], acc, ALU.mult, ALU.add)
                        S.copy(halo[:, xi, :], raw[:, 512:515], eng="dve")
                        S.act(xsT[:, xi, :], acc, AF.Silu)
            dps = bank()
            for blk in range(4):
                for kc in range(16):
                    S.mm(dps[:, blk * 16:(blk + 1) * 16], hT[:, kc, blk * 128:(blk + 1) * 128], wdt[:, kc, :],
                         start=(kc == 0), stop=(kc == 15))
            dsb = small[:, 208:272]
            S.copy(dsb, dps[:, 0:64], eng="dve")
            return dsb

        def bc(ap2, n):
            return ap2.unsqueeze(2).broadcast_to([ap2.shape[0], ap2.shape[1], n])

        sbctr = [0]

        def sbank():
            sbctr[0] += 1
            return psum[:, 6 + sbctr[0] % 2, :]

        def sbank2():
            return psum[:, 6:8, :].rearrange("p a b -> p (a b)")

        def ssd_gen(ti, mode, dps):
            main = mode == "main"
            if main:
                S.memset(Mbd, 0.0, eng="pool")
            for blk in range(4):
                pb = sbank().bitcast(BF16)
                for fc in range(8):
                    S.tr(pb[:, fc * 128:(fc + 1) * 128], xsT[:, fc, blk * 128:(blk + 1) * 128], ident16)
                S.copy(xtok[:, blk, :], pb)
                pb2 = sbank().bitcast(BF16)
                for g in range(2):
                    S.tr(pb2[:, g * 128:(g + 1) * 128], xsT[:, 8 + g, blk * 128:(blk + 1) * 128], ident16)
                S.copy(Btok[:, blk, :], pb2[:, 0:256])
                yield
            if main:
                for c in range(2):
                    S.tt(CTpad[:, c, :, :], xsT[:, 10:12, :],
                         cmask[:, c * 512:(c + 1) * 512].unsqueeze(1).broadcast_to([128, 2, 512]), ALU.mult, eng="pool")
            dtv, ev, dt_, adt = small[:, 0:16], small[:, 16:32], small[:, 32:48], small[:, 48:64]
            cs4 = small[:, 64:128]
            dl, dec, cdAB, ecs = small[:, 128:144], small[:, 144:160], small[:, 160:192], small[:, 192:208]
            St3 = St[:, :].rearrange("p (r c) -> p r c", r=16)
            r16 = lambda ap: ap.rearrange("p (r c) -> p r c", r=16)
            for blk in range(4):
                tk = slice(blk * 128, (blk + 1) * 128)
                S.tt(dtv, dps[:, blk * 16:(blk + 1) * 16], dtb, ALU.add)
                S.act(ev, dtv, AF.Exp)
                S.act(dt_, ev, AF.Ln, bias=1.0)
                S.tt(adt, dt_, a_b[:], ALU.mult)
                yield
                cp = sbank()
                S.mm(cp[:, 0:16], BT2, adt)
                S.mm(cp[:, 16:32], blockones, adt)
                S.mm(cp[:, 32:48], selA, adt)
                S.mm(cp[:, 48:64], selB, adt)
                S.copy(cs4, cp[:, 0:64], eng="dve")
                S.tt(dl, cs4[:, 16:32], cs4[:, 0:16], ALU.subtract)
                S.act(dec, dl, AF.Exp)
                S.act(cdAB, cs4[:, 32:64], AF.Exp)
                if main:
                    S.act(ecs, cs4[:, 0:16], AF.Exp)
                yield
                xtok3 = r16(xtok[:, blk, :])
                S.tt(r16(xdt), xtok3, bc(dt_, 64), ALU.mult, eng="pool")
                S.tt(r16(xdtw), r16(xdt), bc(dec, 64), ALU.mult, eng="pool")
                yield
                if main:
                    cbp = sbank()
                    for g in range(2):
                        S.mm(cbp[:, g * 128:(g + 1) * 128], xsT[:, 8 + g, tk], xsT[:, 10 + g, tk])
                    cb3 = cbp[:, 0:256].rearrange("p (g l) -> p g l", g=2)
                    S.copy(cbsb[0:64, :, :], cb3[0:64, :, 0:64], eng="dve")
                    S.copy(cbsb[64:128, :, :], cb3[64:128, :, 64:128], eng="dve")
                    rhsD = TA
                    S.tt(r16(rhsD), tri2.unsqueeze(1).broadcast_to([128, 16, 64]), bc(adt, 64), ALU.mult, eng="pool")
                    yield
                    dp = sbank2()
                    for h2 in range(2):
                        S.mm(dp[:, h2 * 512:(h2 + 1) * 512], blockones, rhsD[:, h2 * 512:(h2 + 1) * 512], True, False)
                        S.mm(dp[:, h2 * 512:(h2 + 1) * 512], ident32, negmask[:, h2 * 512:(h2 + 1) * 512], False, True)
                    darg = TB
                    S.tt(r16(darg), r16(dp), bc(cs4[:, 0:16], 64), ALU.subtract)
                    S.act(seg, darg, AF.Exp)
                    yield
                    seg4 = seg.rearrange("p (g r c) -> p g r c", g=2, r=8)
                    Mbd4 = Mbd.rearrange("p (g r) c -> p g r c", g=2)
                    for c in range(2):
                        pr = slice(c * 64, (c + 1) * 64)
                        S.tt(Mbd4[pr, :, :, c * 64:(c + 1) * 64], seg4[pr],
                             cbsb[pr, :, :].unsqueeze(2).broadcast_to([64, 2, 8, 64]), ALU.mult)
                    yp = sbank2()
                    for r in range(16):
                        S.mm(yp[:, r * 64:(r + 1) * 64], Mbd[:, r, :], xdt[:, r * 64:(r + 1) * 64])
                    t1 = TA
                    S.copy(t1, yp, eng="act")
                    S.copy(Sbf[0][:], St[:], eng="act")
                    yield
                spA = sbank2()
                for g in range(2):
                    S.mm(spA[:, g * 512:(g + 1) * 512], Btok[0:64, blk, g * 128:(g + 1) * 128], xdtw[0:64, g * 512:(g + 1) * 512])
                S.tt(St3, St3, bc(cdAB[:, 0:16], 64), ALU.mult, eng="pool")
                S.tt(St[:], St[:], spA, ALU.add)
                yield
                if main:
                    S.copy(Sbf[1][:], St[:], eng="act")
                    op_ = sbank2()
                    for g in range(2):
                        S.mm(op_[:, g * 512:(g + 1) * 512], CTpad[:, 0, g, tk], Sbf[0][:, g * 512:(g + 1) * 512], True, False)
                        S.mm(op_[:, g * 512:(g + 1) * 512], CTpad[:, 1, g, tk], Sbf[1][:, g * 512:(g + 1) * 512], False, True)
                    t2 = TB
                    S.tt(r16(t2), r16(op_), bc(ecs, 64), ALU.mult)
                    S.tt(t1, t1, t2, ALU.add)
                    yield
                spB = sbank2()
                for g in range(2):
                    S.mm(spB[:, g * 512:(g + 1) * 512], Btok[64:128, blk, g * 128:(g + 1) * 128],
                         xdtw[64:128, g * 512:(g + 1) * 512])
                S.tt(St3, St3, bc(cdAB[:, 16:32], 64), ALU.mult, eng="pool")
                S.tt(St[:], St[:], spB, ALU.add)
                yield
                if main:
                    S.tt(r16(t2), xtok3, bc(dskip, 64), ALU.mult, eng="pool")
                    S.tt(t1, t1, t2, ALU.add, eng="pool")
                    if blk == 0 and ti == 0:
                        dbg("ytok", t1, [128, 1024])
                    tp = sbank2()
                    for fc in range(8):
                        S.tr(tp[:, fc * 128:(fc + 1) * 128], t1[:, fc * 128:(fc + 1) * 128], ident32)
                    S.tt(gyT[:, :, tk], tp.rearrange("p (a b) -> p a b", a=8), szT[:, :, tk], ALU.mult)
                    yield

        def attention(ti, first_main, steps):
            cur = ti % 2
            prev = 1 - cur
            its = [(hp, par, j) for hp in range(8) for par in range(2) for j in range(par, 8, 2)]
            N = len(its)

            def rows_of(par, m):
                if par == 0 and m == 4:
                    return slice(0, 64)
                if par == 1 and m == 0:
                    return slice(64, 128)
                return slice(0, 128)

            def stageA(i):
                hp, par, j = its[i]
                grp = i // 4
                bb = biasb[grp % 2]
                if i % 4 == 0:
                    S.dma("sp", bb, biasT_d[hp][:, par * 640:(par + 1) * 640])
                b0 = j // 2
                sps = psum[:, 2 * (i % 2):2 * (i % 2) + 2, :].rearrange("p a b -> p (a b)")
                for m in range(5):
                    b = b0 + m
                    kb = Kb[prev] if b < 4 else Kb[cur]
                    S.mm(sps[:, m * 128:(m + 1) * 128], kb[:, hp, (b % 4) * 128:(b % 4 + 1) * 128], qbd[:, hp, j, :])
                sbb = sbt[i % 3]
                S.tt(sbb, sps[:, 0:640], bb, ALU.add)
                pT = pTt[i % 4]
                npv = 4 - b0
                if first_main:
                    S.act(pT[:, 0:npv * 128], sbb[:, 0:npv * 128], AF.Exp, bias=flag[:, 1:2])
                    S.act(pT[:, npv * 128:640], sbb[:, npv * 128:640], AF.Exp)
                else:
                    S.act(pT, sbb, AF.Exp)

            def stageB(i):
                hp, par, j = its[i]
                b0 = j // 2
                pT = pTt[i % 4]
                ob = psum[:, 4 + i % 2, :]
                for m in range(5):
                    b = b0 + m
                    vb = Vb[prev] if b < 4 else Vb[cur]
                    rows = rows_of(par, m)
                    S.mm(ob[:, 0:128], vb[rows, b % 4, hp * 128:(hp + 1) * 128], pT[rows, m * 128:(m + 1) * 128],
                         start=(m == 0), stop=(m == 4))
                for m in range(5):
                    rows = rows_of(par, m)
                    S.mm(ob[:, 128:256], ones16[rows, :], pT[rows, m * 128:(m + 1) * 128], start=(m == 0), stop=(m == 4))
                rc = recf[i % 2]
                S.add("dve", lambda e, rc=rc, ob=ob: e.reciprocal(rc, ob[:, 128:256]), [ob[:, 128:256]], [rc])
                S.tt(attT[0:64, hp, j * 64:(j + 1) * 64], ob[0:64, 0:64], rc[0:64, 0:64], ALU.mult)
                S.tt(attT[64:128, hp, j * 64:(j + 1) * 64], ob[64:128, 64:128], rc[64:128, 64:128], ALU.mult)

            LAG = 3
            PAT = "110"
            FIRST = False
            for i in range(N + LAG):
                npull = int(PAT[i % len(PAT)])
                if FIRST:
                    for _ in range(npull):
                        next(steps, None)
                if i < N:
                    stageA(i)
                if i >= LAG:
                    stageB(i - LAG)
                if not FIRST:
                    for _ in range(npull):
                        next(steps, None)
            for _ in steps:
                pass

        def mixer_out():
            stats(attT, 8, onesH, rstd)
            stats(gyT, 8, onesH, rstd2)
            for fc in range(8):
                S.stt(mixT[:, fc, :], attT[:, fc, :], gatt[:, fc:fc + 1], rstd, ALU.mult, ALU.mult)
            for fc in range(8):
                S.stt(mixT[:, 8 + fc, :], gyT[:, fc, :], gssd[:, fc:fc + 1], rstd2, ALU.mult, ALU.mult)
            wov = w_out.rearrange("(kc p) n -> p kc n", p=128)
            for u in range(8):
                wb = wload(wov[:, :, u * 256:(u + 1) * 256], (16, 256))
                for c2 in range(2):
                    c = u * 2 + c2
                    pb = bank()
                    for kc in range(16):
                        S.mm(pb, wb[:, kc, c2 * 128:(c2 + 1) * 128], mixT[:, kc, :], start=(kc == 0), stop=(kc == 15))
                    S.stt(xT[:, c, :], pb, gt1[:, c:c + 1], xT[:, c, :], ALU.mult, ALU.add)

        wgv = w_gate.rearrange("(kc p) n -> p kc n", p=128)
        wuv = w_up.rearrange("(kc p) n -> p kc n", p=128)

        def ffn():
            stats(xT, 16, onesD, rstd)
            make_h(rstd, a2, sh2)
            for g in range(11):
                at = actT[g % 2]
                for half in range(2):
                    u = g * 2 + half
                    wg = wload(wgv[:, :, u * 256:(u + 1) * 256], (16, 256))
                    wu = wload(wuv[:, :, u * 256:(u + 1) * 256], (16, 256))
                    for c2 in range(2):
                        gp = bank()
                        for kc in range(16):
                            S.mm(gp, wg[:, kc, c2 * 128:(c2 + 1) * 128], hT[:, kc, :], start=(kc == 0), stop=(kc == 15))
                        up = bank()
                        for kc in range(16):
                            S.mm(up, wu[:, kc, c2 * 128:(c2 + 1) * 128], hT[:, kc, :], start=(kc == 0), stop=(kc == 15))
                        sg = sgt[(half * 2 + c2) % 4]
                        S.act(sg, gp, AF.Silu)
                        S.tt(at[:, half * 2 + c2, :], sg, up, ALU.mult)
                for oh in range(2):
                    wd = wload(w_down[g * 512:(g + 1) * 512, oh * 1024:(oh + 1) * 1024].rearrange("(kc p) n -> p kc n", p=128),
                               (4, 1024))
                    for o8 in range(8):
                        oc = oh * 8 + o8
                        pb = bank()
                        for kc in range(4):
                            S.mm(pb, wd[:, kc, o8 * 128:(o8 + 1) * 128], at[:, kc, :], start=(kc == 0), stop=(kc == 3))
                        S.stt(xT[:, oc, :], pb, gt2[:, oc:oc + 1], xT[:, oc, :], ALU.mult, ALU.add)

        def final_out(tm):
            stats(xT, 16, onesD, rstd)
            for kc in range(16):
                S.stt(xT[:, kc, :], xT[:, kc, :], gfin[:, kc:kc + 1], rstd, ALU.mult, ALU.mult)
            for blk in range(4):
                for half in range(2):
                    st = xst[(blk * 2 + half) % 3]
                    for g in range(2):
                        pb = bank()
                        for i in range(4):
                            kc = half * 8 + g * 4 + i
                            S.tr(pb[:, i * 128:(i + 1) * 128], xT[:, kc, blk * 128:(blk + 1) * 128], ident32)
                        S.copy(st[:, g * 512:(g + 1) * 512], pb)
                    S.dma("sp", y_d[tm * T + blk * 128:tm * T + (blk + 1) * 128, half * 1024:(half + 1) * 1024], st)

        front_done = set()
        for ti in range(PRE + NT):
            mode = "main" if ti >= PRE else ("prelast" if ti == PRE - 1 else "pre")
            if ti not in front_done:
                for _ in front_gen(ti):
                    pass
            if ti == 0:
                dbg("hT", hT, [128, 16, 512])
            dps = inproj(ti, mode)
            if ti == 0:
                dbg("xsT", xsT, [128, 12, 512])
                dbg("szT", szT, [128, 8, 512])
                dbg("qbd", qbd, [128, 8, 8, 128])
                dbg("K", Kb[0][:], [128, 8, 512])
                dbg("V", Vb[0][:], [128, 4, 1024])
            steps = ssd_gen(ti, mode, dps)
            if mode != "main":
                nxt = front_gen(ti + 1)
                front_done.add(ti + 1)
                a_live, b_live = True, True
                while a_live or b_live:
                    if a_live:
                        a_live = next(steps, "END") != "END"
                    if b_live:
                        b_live = next(nxt, "END") != "END"
            if ti == PRE - 1:
                S.ts(St[:], St[:], flag[:, 0:1], None, ALU.mult)
                S.ts(halo[:], halo[:], flag[:, 0:1], None, ALU.mult)
            if mode == "main":
                for _ in range(4):
                    next(steps)
                attention(ti, ti == PRE, steps)
                if ti == PRE:
                    dbg("gyT", gyT, [128, 8, 512])
                if ti == PRE:
                    dbg("attT", attT, [128, 8, 512])
                mixer_out()
                if ti == PRE:
                    dbg("x1", xT[:], [128, 16, 512])
                ffn()
                final_out(ti - PRE)

        S.finalize()
        with nc.Block() as block:
            @block.tensor
            def _(e):
                S.emit("pe", e)

            @block.scalar
            def _(e):
                S.emit("act", e)

            @block.vector
            def _(e):
                S.emit("dve", e)

            @block.gpsimd
            def _(e):
                S.emit("pool", e)

            @block.sync
            def _(e):
                S.emit("sp", e)
                for sem, val in S.dval.items():
                    e.wait_ge(sem, val)
    return nc, list(dbg_d.keys())


def host_consts():
    import ml_dtypes
    c32 = np.zeros((128, 5 * 128 + 64 + 1024), np.float32)
    t = np.arange(128)
    c32[:, 0:128] = np.eye(128)
    same = (t[:, None] // 64) == (t[None, :] // 64)
    c32[:, 128:256] = same
    c32[:, 256:384] = same & (t[:, None] <= t[None, :])
    c32[:, 384:512] = (t[:, None] < 64) * np.ones((1, 128))
    c32[:, 512:640] = (t[:, None] >= 64) * np.ones((1, 128))
    l = np.arange(64)
    c32[:, 640:704] = (t[:, None] % 64) <= l[None, :]
    nm = np.where(l[None, :] < (t[:, None] % 64), NEG, 0.0)
    c32[:, 704:1728] = np.tile(nm, (1, 16))
    c16 = np.zeros((128, 4 * 128 + 1024), np.float32)
    c16[:, 0:128] = np.eye(128)
    c16[:, 128:256] = 1.0
    c16[:, 256:384] = 1.0 / 2048
    c16[:, 384:512] = 1.0 / 1024
    tok = np.arange(512)
    c16[:, 512:1024] = ((tok // 64) % 2 == 0)[None, :]
    c16[:, 1024:1536] = ((tok // 64) % 2 == 1)[None, :]
    return c32, c16.astype(ml_dtypes.bfloat16)


def host_bias(rel_bias):
    rb = np.asarray(rel_bias, np.float32)
    p = np.arange(128)[:, None, None, None]
    par = np.arange(2)[None, :, None, None]
    m = np.arange(5)[None, None, :, None]
    q = np.arange(64)[None, None, None, :]
    rel = 512 + 64 * par + q - 128 * m - p
    idx = np.clip(rel, -63, 256) + 63
    out = np.zeros((8, 128, 2, 5, 2, 64), np.float32)
    for hp in range(8):
        for hh in range(2):
            out[hp, :, :, :, hh, :] = rb[2 * hp + hh][idx]
    return out.reshape(8, 128, 1280)


def pcol(v, n):
    return np.ascontiguousarray(np.asarray(v, np.float32).reshape(n, 128).T)


def make_inmaps(inputs, NT, PRE, cores):
    c32, c16 = host_consts()
    biasT = host_bias(inputs["rel_bias"][0])
    pvec = np.zeros((128, NPV), np.float32)
    pvec[:, 0:16] = pcol(inputs["g_mix"][0], 16)
    pvec[:, 16:32] = pcol(inputs["g_ffn"][0], 16)
    pvec[:, 32:48] = pcol(inputs["g_final"], 16)
    pvec[:, PV_CB:PV_CB + 12] = pcol(inputs["conv_b"][0], 12)
    cw = np.asarray(inputs["conv_w"][0], np.float32)
    for xi in range(12):
        for j in range(4):
            pvec[:, PV_CW + xi * 4 + j] = cw[j, xi * 128:(xi + 1) * 128]
    pvec[:, PV_GATT:PV_GATT + 8] = pcol(inputs["g_att_out"][0], 8)
    pvec[:, PV_GSSD:PV_GSSD + 8] = pcol(inputs["g_ssd_out"][0], 8)
    pvec[:, PV_BADA:PV_BADA + 96] = pcol(inputs["b_ada"][0], 96)
    bvec = np.zeros((128, 48), np.float32)
    bvec[:, 0:16] = np.asarray(inputs["dt_bias"][0], np.float32)[None, :]
    bvec[:, 16:32] = np.asarray(inputs["a_log"][0], np.float32)[None, :]
    bvec[:, 32:48] = np.asarray(inputs["d_skip"][0], np.float32)[None, :]
    shared = dict(
        w_ada=np.ascontiguousarray(inputs["w_ada"][0]), w_in=np.ascontiguousarray(inputs["w_in"][0]),
        w_out=np.ascontiguousarray(inputs["w_out"][0]), w_gate=np.ascontiguousarray(inputs["w_gate"][0]),
        w_up=np.ascontiguousarray(inputs["w_up"][0]), w_down=np.ascontiguousarray(inputs["w_down"][0]),
        pvec=pvec, bvec=bvec, biasT=biasT, c32=c32, c16=c16)
    x = inputs["x"]
    maps = []
    for (b, t0) in cores:
        xcat = np.zeros(((PRE + NT) * T, 2048), np.float32)
        has_prev = t0 > 0
        if has_prev and PRE > 0:
            xcat[0:PRE * T] = x[b, t0 - PRE * T:t0]
        xcat[PRE * T:] = x[b, t0:t0 + NT * T]
        flag = np.zeros((128, 2), np.float32)
        flag[:, 0] = 1.0 if has_prev else 0.0
        flag[:, 1] = 0.0 if has_prev else NEG
        m = dict(shared)
        m.update(xc=xcat, cvec=pcol(inputs["c"][b], 16), flag=flag)
        maps.append(m)
    return maps


_CACHE = {}


def kernel(**inputs):
    inputs = {k: np.asarray(v) for k, v in inputs.items()}
    NT, PRE = 8, 8
    cores = [(b, h * 4096) for b in range(4) for h in range(2)]
    if "nc" not in _CACHE:
        _CACHE["nc"] = build_program(NT, PRE)[0]
    nc = _CACHE["nc"]
    maps = make_inmaps(inputs, NT, PRE, cores)
    res = run_bass_kernel_spmd(nc, maps, core_ids=list(range(8)))
    out = np.zeros((4, 8192, 2048), np.float32)
    for i, (b, t0) in enumerate(cores):
        out[b, t0:t0 + NT * T] = res.results[i]["y"]
    return out
```
